# Optimizing a Trainium2 kernel written in Bass

```python
import jax, jax.numpy as jnp
from jax import lax
import numpy as np

D_MODEL = 1024
BATCH = 32
SEQ = 256
DEPTH = 2
DEC_BATCH = 2
DEC_SEQ = 1024
PAST_LEN = 256

GRID_W = 64
HEAD_DIM = 64
N_EVEN = (DEPTH + 1) // 2
N_ODD = DEPTH // 2
H_A = 8
Q_RANK = 256
KV_RANK = 128
NOPE_A = 64
ROPE_A = 32
V_A = 64
QK_A = NOPE_A + ROPE_A
H_B = 8
NA_ROWS = 8
NA_COLS = 16
H_C = 8
KV_C = 2
H_D = 8
KV_D = 2
SWA_HALF = 128
QBLOCK = 128
ROPE_THETA = 10000.0
EPS = 1e-6
NEG_INF = -1e30
EVEN_SPLITS = (Q_RANK, KV_RANK, ROPE_A, H_A * V_A, H_B * HEAD_DIM, H_B * HEAD_DIM, H_B * HEAD_DIM, H_B * HEAD_DIM)
EVEN_IN = sum(EVEN_SPLITS)
EVEN_MIX = H_A * V_A + H_B * HEAD_DIM
ODD_SPLITS = (H_C * HEAD_DIM, KV_C * HEAD_DIM, KV_C * HEAD_DIM, H_C * HEAD_DIM, H_D * HEAD_DIM, KV_D * HEAD_DIM, KV_D * HEAD_DIM, H_D * HEAD_DIM)
ODD_IN = sum(ODD_SPLITS)
ODD_MIX = H_C * HEAD_DIM + H_D * HEAD_DIM

kernel_name = 'hybrid_diffusion_prefix_trunk_step'


def _split(z, sizes):
    out, off = [], 0
    for s in sizes:
        out.append(z[..., off:off + s])
        off += s
    return out


def _rmsnorm(x, g):
    xf = x.astype(jnp.float32)
    y = xf * lax.rsqrt(jnp.mean(xf * xf, axis=-1, keepdims=True) + EPS)
    return (y * g.astype(jnp.float32)).astype(x.dtype)


def _heads(x, n):
    b, s, _ = x.shape
    return x.reshape(b, s, n, -1).transpose(0, 2, 1, 3)


def _merge(o):
    b, h, s, d = o.shape
    return o.transpose(0, 2, 1, 3).reshape(b, s, h * d)


def _groups(q, kvh):
    b, h, s, d = q.shape
    return q.reshape(b, kvh, h // kvh, s, d)


def _axial_rope(s, rot_dim):
    quarter = rot_dim // 4
    t = jnp.arange(s)
    inv = ROPE_THETA ** (-jnp.arange(quarter, dtype=jnp.float32) / quarter)
    row = (t // GRID_W).astype(jnp.float32)[:, None] * inv
    col = (t % GRID_W).astype(jnp.float32)[:, None] * inv
    ang = jnp.concatenate([row, col], axis=-1)
    return jnp.cos(ang), jnp.sin(ang)


def _rope(x, cos, sin):
    half = x.shape[-1] // 2
    x1 = x[..., :half].astype(jnp.float32)
    x2 = x[..., half:].astype(jnp.float32)
    return jnp.concatenate([x1 * cos - x2 * sin, x1 * sin + x2 * cos], axis=-1).astype(x.dtype)


def _rope_tail(x, cos, sin, n):
    return jnp.concatenate([x[..., :-n], _rope(x[..., -n:], cos, sin)], axis=-1)


def _modulate(x, cond, g, w_mod, b_mod):
    m = jax.nn.silu(cond) @ w_mod + b_mod
    if m.ndim == 2:
        m = m[:, None, :]
    shift, scale, gate = jnp.split(m, 3, axis=-1)
    return _rmsnorm(x, g) * (1 + scale) + shift, gate


def _attend_blocked(q, parts, sink=None):
    b, kh, g, s, dk = q.shape
    bq = min(QBLOCK, s)
    nb = s // bq
    qb = jnp.moveaxis(q.reshape(b, kh, g, nb, bq, dk), 3, 0)
    lens = [k.shape[2] for k, _ in parts]

    def block(qi):
        sc = jnp.concatenate([jnp.einsum('bkgqd,bkld->bkgql', qi, k) for k, _ in parts], axis=-1).astype(jnp.float32)
        if sink is not None:
            sk = jnp.broadcast_to(sink.astype(jnp.float32)[None, :, :, None, None], sc.shape[:-1] + (1,))
            sc = jnp.concatenate([sc, sk], axis=-1)
        p = jax.nn.softmax(sc, axis=-1)
        out, off = None, 0
        for (k, v), ln in zip(parts, lens):
            o = jnp.einsum('bkgql,bkld->bkgqd', p[..., off:off + ln].astype(v.dtype), v)
            out = o if out is None else out + o
            off += ln
        return out

    o = lax.map(block, qb)
    return jnp.moveaxis(o, 0, 3).reshape(b, kh * g, s, -1)


def _neighbourhood(q, k, v, k_ctx, v_ctx, rpb):
    b, h, s, d = q.shape
    rows = s // GRID_W
    kr = min(NA_ROWS, rows)
    kc = NA_COLS
    ncb = GRID_W // kc
    halo = 2 * kc
    r = jnp.arange(rows)
    row_idx = jnp.clip(r - kr // 2, 0, rows - kr)[:, None] + jnp.arange(kr)[None, :]
    j = jnp.arange(ncb)
    col_idx = jnp.clip(j * kc - kc // 2, 0, GRID_W - halo)[:, None] + jnp.arange(halo)[None, :]
    qcol = j[:, None] * kc + jnp.arange(kc)[None, :]
    cs = jnp.clip(qcol - kc // 2, 0, GRID_W - kc)
    valid = (col_idx[:, None, :] >= cs[..., None]) & (col_idx[:, None, :] < cs[..., None] + kc)
    dr = row_idx - r[:, None] + (NA_ROWS - 1)
    dc = jnp.clip(col_idx[:, None, :] - qcol[..., None] + (kc - 1), 0, 2 * kc - 2)
    bias = rpb[:, dr[:, None, None, :, None], dc[None, :, :, None, :]].astype(jnp.float32)
    bias = jnp.where(valid[None, None, :, :, None, :], bias, NEG_INF)
    qg = q.reshape(b, h, rows, ncb, kc, d)
    ri = row_idx[:, None, :, None]
    ci = col_idx[None, :, None, :]
    kg = k.reshape(b, h, rows, GRID_W, d)[:, :, ri, ci]
    vg = v.reshape(b, h, rows, GRID_W, d)[:, :, ri, ci]
    s_lat = jnp.einsum('bhrjqd,bhrjkwd->bhrjqkw', qg, kg).astype(jnp.float32) + bias[None]
    nl = kr * halo
    s_lat = s_lat.reshape(b, h, rows, ncb, kc, nl)
    s_ctx = jnp.einsum('bhrjqd,bhcd->bhrjqc', qg, k_ctx).astype(jnp.float32)
    p = jax.nn.softmax(jnp.concatenate([s_lat, s_ctx], axis=-1), axis=-1)
    o = (jnp.einsum('bhrjqn,bhrjnd->bhrjqd', p[..., :nl].astype(v.dtype), vg.reshape(b, h, rows, ncb, nl, d))
         + jnp.einsum('bhrjqc,bhcd->bhrjqd', p[..., nl:].astype(v.dtype), v_ctx))
    return o.reshape(b, h, s, d)


def _windowed(q, k, v, k_ctx, v_ctx, sink):
    b, kh, g, s, d = q.shape
    w = SWA_HALF
    nb = s // w
    pad = ((0, 0), (0, 0), (w, w), (0, 0))
    kp = jnp.pad(k, pad).reshape(b, kh, nb + 2, w, d)
    vp = jnp.pad(v, pad).reshape(b, kh, nb + 2, w, d)
    kb = jnp.concatenate([kp[:, :, 0:nb], kp[:, :, 1:nb + 1], kp[:, :, 2:nb + 2]], axis=3)
    vb = jnp.concatenate([vp[:, :, 0:nb], vp[:, :, 1:nb + 1], vp[:, :, 2:nb + 2]], axis=3)
    qb = q.reshape(b, kh, g, nb, w, d)
    qpos = jnp.arange(s).reshape(nb, w)
    kpos = jnp.arange(nb)[:, None] * w - w + jnp.arange(3 * w)[None, :]
    valid = ((kpos[:, None, :] >= 0) & (kpos[:, None, :] < s)
             & (jnp.abs(qpos[:, :, None] - kpos[:, None, :]) <= w))
    s_loc = jnp.where(valid, jnp.einsum('bkgnqd,bknld->bkgnql', qb, kb).astype(jnp.float32), NEG_INF)
    s_ctx = jnp.einsum('bkgnqd,bkcd->bkgnqc', qb, k_ctx).astype(jnp.float32)
    sk = jnp.broadcast_to(sink.astype(jnp.float32)[None, :, :, None, None, None], s_loc.shape[:-1] + (1,))
    p = jax.nn.softmax(jnp.concatenate([s_loc, s_ctx, sk], axis=-1), axis=-1)
    nl = 3 * w
    nc = k_ctx.shape[2]
    o = (jnp.einsum('bkgnql,bknld->bkgnqd', p[..., :nl].astype(v.dtype), vb)
         + jnp.einsum('bkgnqc,bkcd->bkgnqd', p[..., nl:nl + nc].astype(v.dtype), v_ctx))
    return o.reshape(b, kh * g, s, d)


def _gated_out(o1, g1, o2, g2, w_out):
    y = jnp.concatenate([_merge(o1) * jax.nn.silu(g1), _merge(o2) * jax.nn.silu(g2)], axis=-1)
    return y @ w_out


def _even_project(h, w_in, qa_g, w_q_up, kva_g, q_g, na_q_g, na_k_g):
    q_lat, ckv, krope, gate_a, q_b, k_b, v_b, gate_b = _split(h @ w_in, EVEN_SPLITS)
    q_a = _rmsnorm(_heads(_rmsnorm(q_lat, qa_g) @ w_q_up, H_A), q_g)
    q_b = _rmsnorm(_heads(q_b, H_B), na_q_g)
    k_b = _rmsnorm(_heads(k_b, H_B), na_k_g)
    return q_a, _rmsnorm(ckv, kva_g), krope, gate_a, q_b, k_b, _heads(v_b, H_B), gate_b


def _mla_kv(ckv, krope, w_kv_up, k_g):
    kv = _heads(ckv @ w_kv_up, H_A)
    b, h, l, _ = kv.shape
    k = jnp.concatenate([kv[..., :NOPE_A], jnp.broadcast_to(krope[:, None], (b, h, l, ROPE_A))], axis=-1)
    return _rmsnorm(k, k_g), kv[..., NOPE_A:]


def _even_context(h, pe):
    w_in, qa_g, w_q_up, kva_g, w_kv_up, q_g, k_g, na_q_g, na_k_g, rpb, w_out = pe
    q_a, ckv, krope, gate_a, q_b, k_b, v_b, gate_b = _even_project(h, w_in, qa_g, w_q_up, kva_g, q_g, na_q_g, na_k_g)
    k_a, v_a = _mla_kv(ckv, krope, w_kv_up, k_g)
    o_a = _attend_blocked(_groups(q_a * QK_A ** -0.5, H_A), [(k_a, v_a)])
    o_b = _attend_blocked(_groups(q_b * HEAD_DIM ** -0.5, H_B), [(k_b, v_b)])
    return _gated_out(o_a, gate_a, o_b, gate_b, w_out), (ckv, krope, k_b, v_b)


def _even_latent(h, ckv_c, krope_c, k_b_c, v_b_c, pe):
    w_in, qa_g, w_q_up, kva_g, w_kv_up, q_g, k_g, na_q_g, na_k_g, rpb, w_out = pe
    q_a, ckv, krope, gate_a, q_b, k_b, v_b, gate_b = _even_project(h, w_in, qa_g, w_q_up, kva_g, q_g, na_q_g, na_k_g)
    cos, sin = _axial_rope(h.shape[1], ROPE_A)
    q_a = _rope_tail(q_a, cos, sin, ROPE_A)
    k_a, v_a = _mla_kv(ckv, krope, w_kv_up, k_g)
    k_a = _rope_tail(k_a, cos, sin, ROPE_A)
    k_ac, v_ac = _mla_kv(ckv_c, krope_c, w_kv_up, k_g)
    o_a = _attend_blocked(_groups(q_a * QK_A ** -0.5, H_A), [(k_a, v_a), (k_ac, v_ac)])
    o_b = _neighbourhood(q_b * HEAD_DIM ** -0.5, k_b, v_b, k_b_c, v_b_c, rpb)
    return _gated_out(o_a, gate_a, o_b, gate_b, w_out)


def _odd_project(h, w_in, gq_g, gk_g, sq_g, sk_g):
    qc, kc, vc, gc, qd, kd, vd, gd = _split(h @ w_in, ODD_SPLITS)
    return (_rmsnorm(_heads(qc, H_C), gq_g), _rmsnorm(_heads(kc, KV_C), gk_g), _heads(vc, KV_C), gc,
            _rmsnorm(_heads(qd, H_D), sq_g), _rmsnorm(_heads(kd, KV_D), sk_g), _heads(vd, KV_D), gd)


def _odd_context(h, po):
    w_in, gq_g, gk_g, sq_g, sk_g, sink, w_out = po
    qc, kc, vc, gc, qd, kd, vd, gd = _odd_project(h, w_in, gq_g, gk_g, sq_g, sk_g)
    sc = HEAD_DIM ** -0.5
    o_c = _attend_blocked(_groups(qc * sc, KV_C), [(kc, vc)])
    o_d = _attend_blocked(_groups(qd * sc, KV_D), [(kd, vd)], sink.reshape(KV_D, H_D // KV_D))
    return _gated_out(o_c, gc, o_d, gd, w_out), (kc, vc, kd, vd)


def _odd_latent(h, kc_c, vc_c, kd_c, vd_c, po):
    w_in, gq_g, gk_g, sq_g, sk_g, sink, w_out = po
    qc, kc, vc, gc, qd, kd, vd, gd = _odd_project(h, w_in, gq_g, gk_g, sq_g, sk_g)
    cos, sin = _axial_rope(h.shape[1], HEAD_DIM)
    qc, kc, qd, kd = _rope(qc, cos, sin), _rope(kc, cos, sin), _rope(qd, cos, sin), _rope(kd, cos, sin)
    sc = HEAD_DIM ** -0.5
    o_c = _attend_blocked(_groups(qc * sc, KV_C), [(kc, vc), (kc_c, vc_c)])
    o_d = _windowed(_groups(qd * sc, KV_D), kd, vd, kd_c, vd_c, sink.reshape(KV_D, H_D // KV_D))
    return _gated_out(o_c, gc, o_d, gd, w_out)


def setup_inputs(seed: int = 0) -> dict:
    key = jax.random.key(seed)
    ks = iter(jax.random.split(key, 48))

    def nrm(shape, scale):
        return jax.random.normal(next(ks), shape, jnp.float32) * scale

    def gain(shape):
        return 1.0 + nrm(shape, 0.01)

    return {
        'x_prompt': nrm((BATCH, SEQ, D_MODEL), 1.0),
        'x_sample': nrm((DEC_BATCH, DEC_SEQ, D_MODEL), 1.0),
        'cache_mla_ckv': nrm((DEC_BATCH, N_EVEN, PAST_LEN, KV_RANK), 1.0),
        'cache_mla_krope': nrm((DEC_BATCH, N_EVEN, PAST_LEN, ROPE_A), 1.0),
        'cache_na_k': nrm((DEC_BATCH, N_EVEN, H_B, PAST_LEN, HEAD_DIM), 1.0),
        'cache_na_v': nrm((DEC_BATCH, N_EVEN, H_B, PAST_LEN, HEAD_DIM), 1.0),
        'cache_gqa_k': nrm((DEC_BATCH, N_ODD, KV_C, PAST_LEN, HEAD_DIM), 1.0),
        'cache_gqa_v': nrm((DEC_BATCH, N_ODD, KV_C, PAST_LEN, HEAD_DIM), 1.0),
        'cache_swa_k': nrm((DEC_BATCH, N_ODD, KV_D, PAST_LEN, HEAD_DIM), 1.0),
        'cache_swa_v': nrm((DEC_BATCH, N_ODD, KV_D, PAST_LEN, HEAD_DIM), 1.0),
        'c': nrm((DEC_BATCH, D_MODEL), 1.0),
        'c_ctx': nrm((D_MODEL,), 1.0),
        'norm_g': gain((DEPTH, D_MODEL)),
        'w_mod': nrm((DEPTH, D_MODEL, 3 * D_MODEL), 0.5 * D_MODEL ** -0.5),
        'b_mod': nrm((DEPTH, 3 * D_MODEL), 0.01),
        'w_in_even': nrm((N_EVEN, D_MODEL, EVEN_IN), D_MODEL ** -0.5),
        'mla_qa_g': gain((N_EVEN, Q_RANK)),
        'w_q_up': nrm((N_EVEN, Q_RANK, H_A * QK_A), Q_RANK ** -0.5),
        'mla_kva_g': gain((N_EVEN, KV_RANK)),
        'w_kv_up': nrm((N_EVEN, KV_RANK, H_A * (NOPE_A + V_A)), KV_RANK ** -0.5),
        'mla_q_g': gain((N_EVEN, QK_A)),
        'mla_k_g': gain((N_EVEN, QK_A)),
        'na_q_g': gain((N_EVEN, HEAD_DIM)),
        'na_k_g': gain((N_EVEN, HEAD_DIM)),
        'na_rpb': nrm((N_EVEN, H_B, 2 * NA_ROWS - 1, 2 * NA_COLS - 1), 0.1),
        'w_out_even': nrm((N_EVEN, EVEN_MIX, D_MODEL), EVEN_MIX ** -0.5),
        'w_in_odd': nrm((N_ODD, D_MODEL, ODD_IN), D_MODEL ** -0.5),
        'gqa_q_g': gain((N_ODD, HEAD_DIM)),
        'gqa_k_g': gain((N_ODD, HEAD_DIM)),
        'swa_q_g': gain((N_ODD, HEAD_DIM)),
        'swa_k_g': gain((N_ODD, HEAD_DIM)),
        'swa_sink': nrm((N_ODD, H_D), 1.0),
        'w_out_odd': nrm((N_ODD, ODD_MIX, D_MODEL), ODD_MIX ** -0.5),
    }


def reference(x_prompt, x_sample, cache_mla_ckv, cache_mla_krope, cache_na_k, cache_na_v, cache_gqa_k, cache_gqa_v,
              cache_swa_k, cache_swa_v, c, c_ctx, norm_g, w_mod, b_mod, w_in_even, mla_qa_g, w_q_up, mla_kva_g,
              w_kv_up, mla_q_g, mla_k_g, na_q_g, na_k_g, na_rpb, w_out_even, w_in_odd, gqa_q_g, gqa_k_g, swa_q_g,
              swa_k_g, swa_sink, w_out_odd):
    xp, xs = x_prompt, x_sample
    st_e = ([], [], [], [])
    st_o = ([], [], [], [])
    for l in range(DEPTH):
        i = l // 2
        hp, gp = _modulate(xp, c_ctx, norm_g[l], w_mod[l], b_mod[l])
        hs, gs = _modulate(xs, c, norm_g[l], w_mod[l], b_mod[l])
        if l % 2 == 0:
            pe = (w_in_even[i], mla_qa_g[i], w_q_up[i], mla_kva_g[i], w_kv_up[i], mla_q_g[i], mla_k_g[i],
                  na_q_g[i], na_k_g[i], na_rpb[i], w_out_even[i])
            yp, ctx = _even_context(hp, pe)
            ys = _even_latent(hs, cache_mla_ckv[:, i], cache_mla_krope[:, i], cache_na_k[:, i], cache_na_v[:, i], pe)
            for lst, t in zip(st_e, ctx):
                lst.append(t)
        else:
            po = (w_in_odd[i], gqa_q_g[i], gqa_k_g[i], swa_q_g[i], swa_k_g[i], swa_sink[i], w_out_odd[i])
            yp, ctx = _odd_context(hp, po)
            ys = _odd_latent(hs, cache_gqa_k[:, i], cache_gqa_v[:, i], cache_swa_k[:, i], cache_swa_v[:, i], po)
            for lst, t in zip(st_o, ctx):
                lst.append(t)
        xp = xp + gp * yp
        xs = xs + gs * ys
    new_mla_ckv = jnp.stack(st_e[0], axis=1)
    new_mla_krope = jnp.stack(st_e[1], axis=1)
    new_na_k = jnp.stack(st_e[2], axis=1)
    new_na_v = jnp.stack(st_e[3], axis=1)
    new_gqa_k = jnp.stack(st_o[0], axis=1)
    new_gqa_v = jnp.stack(st_o[1], axis=1)
    new_swa_k = jnp.stack(st_o[2], axis=1)
    new_swa_v = jnp.stack(st_o[3], axis=1)
    return (xp, xs, new_mla_ckv, new_mla_krope, new_na_k, new_na_v, new_gqa_k, new_gqa_v, new_swa_k, new_swa_v)
```

```python
import contextlib
import os
import numpy as np
import ml_dtypes
import concourse.bass as bass
import concourse.mybir as mybir
from concourse.bass_utils import run_bass_kernel_spmd
from concourse.ap import AP

F32 = mybir.dt.float32
BF16 = mybir.dt.bfloat16
ALU = mybir.AluOpType
ACTF = mybir.ActivationFunctionType
AX = mybir.AxisListType

NCORES = 8
D = 1024
EPS = 1e-6
EV_IN = 2976
OD_IN = 2560
NPS = 4
SEQ = 256
SS = 1024
PAST = 256


class DSem:
    __slots__ = ("idx", "count")

    def __init__(self, idx):
        self.idx = idx
        self.count = 0


class Buf:
    __slots__ = ("name", "writer", "readers", "dq")

    def __init__(self, name):
        self.name = name
        self.writer = None
        self.readers = []
        self.dq = {}


class Sched:
    ENG = ("pe", "act", "dve", "pool", "sp")

    def __init__(self, nc):
        self.nc = nc
        self.ops = {e: [] for e in self.ENG}
        self.waited = {e: {} for e in self.ENG}
        self.dsems = []
        self.final_dma = []
        self.baton = None
        self._tls = None

    def stream_id(self):
        return getattr(self._tls, "me", None) if self._tls is not None else None

    def interleave(self, fa, fb, na=1, nb=1):
        import threading
        sem = [threading.Semaphore(0), threading.Semaphore(0)]
        done = [False, False]
        quota = [na, nb]
        cnt = [0]
        cur = [0]
        err = []

        def switch(me):
            other = 1 - me
            if done[other]:
                return
            cnt[0] += 1
            if cnt[0] >= quota[me]:
                cnt[0] = 0
                cur[0] = other
                sem[other].release()
                sem[me].acquire()

        def runner(me, f):
            sem[me].acquire()
            try:
                f()
            except BaseException as ex:
                err.append(ex)
            done[me] = True
            cur[0] = 1 - me
            sem[1 - me].release()

        tls = threading.local()
        self._tls = tls

        def baton():
            me = getattr(tls, "me", None)
            if me is not None:
                switch(me)

        def wrap(me, f):
            def g():
                tls.me = me
                f()
            return g

        ta = threading.Thread(target=runner, args=(0, wrap(0, fa)))
        tb_ = threading.Thread(target=runner, args=(1, wrap(1, fb)))
        old = self.baton
        self.baton = baton
        ta.start()
        tb_.start()
        sem[0].release()
        ta.join()
        tb_.join()
        self.baton = old
        self._tls = None
        if err:
            raise err[0]

    def _need(self, eng, tok, waits):
        if tok is None:
            return
        if tok[0] == "e":
            _, src, idx = tok
            if src == eng and eng == "pe":
                return
            key = ("e", src)
            val = idx
        else:
            _, b, val = tok
            key = ("d", id(b))
        w = self.waited[eng]
        if w.get(key, -1) >= val:
            return
        w[key] = val
        waits.append(tok)
        if tok[0] == "e":
            self.ops[tok[1]][tok[2]]["signal"] = True

    def _deps(self, eng, reads, writes):
        toks = []
        for b in reads:
            toks.append(b.writer)
        for b in writes:
            toks.append(b.writer)
            toks.extend(b.readers)
        best = {}
        for t in toks:
            if t is None:
                continue
            key = ("e", t[1]) if t[0] == "e" else ("d", id(t[1]))
            if key not in best or t[2] > best[key][2]:
                best[key] = t
        waits = []
        for t in best.values():
            self._need(eng, t, waits)
        return waits

    def op(self, eng, fn, reads=(), writes=()):
        if self.baton is not None:
            self.baton()
        waits = self._deps(eng, reads, writes)
        idx = len(self.ops[eng])
        self.ops[eng].append(dict(fn=fn, waits=waits, signal=False, dma=None))
        tok = ("e", eng, idx)
        for b in reads:
            b.readers.append(tok)
        for b in writes:
            b.writer = tok
            b.readers = []
        return tok

    def dma(self, eng, out, in_, owner, reads=(), writes=(), final=False, **kw):
        if self.baton is not None:
            self.baton()
        waits = self._deps(eng, reads, writes)
        ds = owner.dq.get(eng)
        if ds is None:
            ds = DSem(len(self.dsems))
            owner.dq[eng] = ds
            self.dsems.append(ds)
        ds.count += 16
        tok = ("d", ds, ds.count)

        def fn(e, out=out, in_=in_, kw=kw):
            return e.dma_start(out=out, in_=in_, **kw)

        self.ops[eng].append(dict(fn=fn, waits=waits, signal=False, dma=ds))
        for b in reads:
            b.readers.append(tok)
        for b in writes:
            b.writer = tok
            b.readers = []
        if final:
            self.final_dma.append(tok)
        return tok

    def dma_custom(self, eng, fn, owner, reads=(), writes=(), final=False):
        waits = self._deps(eng, reads, writes)
        ds = owner.dq.get(eng)
        if ds is None:
            ds = DSem(len(self.dsems))
            owner.dq[eng] = ds
            self.dsems.append(ds)
        ds.count += 16
        tok = ("d", ds, ds.count)
        self.ops[eng].append(dict(fn=fn, waits=waits, signal=False, dma=ds))
        for b in reads:
            b.readers.append(tok)
        for b in writes:
            b.writer = tok
            b.readers = []
        if final:
            self.final_dma.append(tok)
        return tok

    def emit(self):
        nc = self.nc
        with contextlib.ExitStack() as st:
            esem = {e: st.enter_context(nc.semaphore("es_" + e)) for e in self.ENG}
            dsem = [st.enter_context(nc.semaphore("ds_%d" % i)) for i in range(len(self.dsems))]
            block = st.enter_context(nc.Block())
            ordinal = {}
            for e in self.ENG:
                c = 0
                for i, o in enumerate(self.ops[e]):
                    if o["signal"]:
                        c += 1
                        ordinal[(e, i)] = c
            finals = list(self.final_dma)

            def run(e, h):
                for i, o in enumerate(self.ops[e]):
                    for t in o["waits"]:
                        if t[0] == "e":
                            h.wait_ge(esem[t[1]], ordinal[(t[1], t[2])])
                        else:
                            h.wait_ge(dsem[t[1].idx], t[2])
                    ins = o["fn"](h)
                    if o["dma"] is not None:
                        ins.then_inc(dsem[o["dma"].idx], 16)
                    elif o["signal"]:
                        ins.then_inc(esem[e], 1)
                if e == "sp":
                    done = {}
                    for t in finals:
                        done[t[1].idx] = max(done.get(t[1].idx, 0), t[2])
                    for k, v in done.items():
                        h.wait_ge(dsem[k], v)

            @block.tensor
            def _(h):
                run("pe", h)

            @block.scalar
            def _(h):
                run("act", h)

            @block.vector
            def _(h):
                run("dve", h)

            @block.gpsimd
            def _(h):
                run("pool", h)

            @block.sync
            def _(h):
                run("sp", h)


class Tl:
    def __init__(self, t, name):
        self.t = t
        self.b = Buf(name)

    def __getitem__(self, k):
        return self.t[k]


def bc(ap, shape):
    return ap.broadcast_to(list(shape))


def seg2(ap2d, stride, nseg):
    a = ap2d.ap
    return AP(ap2d.tensor, ap2d.offset, [list(a[0]), [stride, nseg], list(a[1])])


DEBUG = False


def build_program():
    nc = bass.Bass("TRN2", target_bir_lowering=False)
    S = Sched(nc)

    def din(name, shape, dt=F32):
        return nc.dram_tensor(name, list(shape), dt, kind="ExternalInput").ap()

    def dout(name, shape, dt=F32):
        return nc.dram_tensor(name, list(shape), dt, kind="ExternalOutput").ap()

    def dscr(name, shape, dt=F32):
        return nc.dram_tensor(name, list(shape), dt, kind="Internal").ap()

    xp = din("xp", [NPS * SEQ, D])
    xs = din("xs", [SS, D])
    c_s = din("c_s", [D])
    c_ctx = din("c_ctx", [D])
    ca_ckv = din("ca_ckv", [PAST, 128])
    ca_krope = din("ca_krope", [PAST, 32])
    ca_nak = din("ca_nak", [8, PAST, 64])
    ca_nav = din("ca_nav", [8, PAST, 64])
    ca_gk = din("ca_gk", [2, PAST, 64])
    ca_gv = din("ca_gv", [2, PAST, 64])
    ca_sk = din("ca_sk", [2, PAST, 64])
    ca_sv = din("ca_sv", [2, PAST, 64])
    norm_g = din("norm_g", [2, D])
    w_mod = din("w_mod", [2, D, 3 * D])
    b_mod = din("b_mod", [2, 3 * D])
    w_in_e = din("w_in_e", [D, EV_IN])
    w_q_up = din("w_q_up", [256, 768])
    w_kv_up = din("w_kv_up", [128, 1024])
    w_out_e = din("w_out_e", [D, D])
    w_in_o = din("w_in_o", [D, OD_IN])
    w_out_o = din("w_out_o", [D, D])
    gains = din("gains", [1, 968])
    rpbT = din("rpbT", [64, 8, 960])
    colvalid = din("colvalid", [128, 64])
    swm_own = din("swm_own", [8, 128, 256], BF16)
    own_idx = din("own_idx", [256, 1], mybir.dt.int32)
    rope_own = din("rope_own", [256, 2, 32])
    rope_e = din("rope_e", [SS, 2, 16])
    rope_o = din("rope_o", [SS, 2, 32])

    yp = dout("yp", [NPS * SEQ, D])
    ys = dout("ys", [256, D])
    n_ckv = dout("n_ckv", [NPS, SEQ, 128])
    n_krope = dout("n_krope", [NPS, SEQ, 32])
    n_nak = dout("n_nak", [NPS, 8, SEQ, 64])
    n_nav = dout("n_nav", [NPS, 8, SEQ, 64])
    n_gk = dout("n_gk", [NPS, 2, SEQ, 64])
    n_gv = dout("n_gv", [NPS, 2, SEQ, 64])
    n_sk = dout("n_sk", [NPS, 2, SEQ, 64])
    n_sv = dout("n_sv", [NPS, 2, SEQ, 64])

    x1p = dscr("x1p", [NPS * SEQ, D])
    x1s = dscr("x1s", [SS, D])
    ttd = dscr("ttd", [128, 8, 960], BF16)
    x1p_b = [Buf("x1p%d" % i) for i in range(NPS * SEQ // 128)]
    x1s_b = [Buf("x1s%d" % i) for i in range(SS // 128)]
    ttd_b = [Buf("ttd%d" % i) for i in range(8)]

    with contextlib.ExitStack() as st:
        def sb(name, shape, dt=F32):
            return Tl(st.enter_context(nc.sbuf_tensor(name, list(shape), dt)), name)

        def ps(name, shape, dt=F32):
            return Tl(st.enter_context(nc.psum_tensor(name, list(shape), dt)), name)

        w_in = sb("w_in", [128, 8, EV_IN], BF16)
        w_in_b = [Buf("w_in%d" % k) for k in range(8)]
        w_out = sb("w_out", [128, 8, D], BF16)
        w_out_b = [Buf("w_out%d" % k) for k in range(8)]
        wq = sb("wq", [128, 2, 768], BF16)
        wkv = sb("wkv", [128, 1024], BF16)

        KnT = sb("KnT", [128, 4, 1280], BF16)
        KrT = sb("KrT", [32, 1280], BF16)
        KbT = sb("KbT", [128, 4, 1280], BF16)
        Vg = sb("Vg", [128, 10, 16, 65], BF16)
        rsk = sb("rsk", [128, 10, 8], F32)
        QnT = sb("QnT", [128, 4, 256], BF16)
        QrT = sb("QrT", [32, 8, 256], BF16)
        QbT = sb("QbT", [128, 4, 256], BF16)
        Gb = sb("Gb", [128, 2, 1024], BF16)
        Ob = sb("Ob", [128, 2, 1024], BF16)
        ogT = sb("ogT", [128, 8, 256], BF16)
        hTb = sb("hTb", [128, 8, 256], BF16)
        xb = [sb("xb%d" % i, [128, D], F32) for i in range(2)]
        xtmp = sb("xtmp", [128, 512], F32)
        zb = [sb("z%d" % i, [128, 2048], F32) for i in range(2)]
        zg = [[Buf("z%d_%d" % (i, g)) for g in range(4)] for i in range(2)]
        U1 = sb("U1", [128, 1024], F32)
        U2 = sb("U2", [128, 1024], F32)
        xsb = sb("xsb", [128, 1024], BF16)
        tb = sb("tb", [128, 1024], BF16)
        tb2 = sb("tb2", [128, 256], BF16)
        ckvT = sb("ckvT", [128, 128], BF16)
        qlT = sb("qlT", [128, 2, 128], BF16)
        ident = sb("ident", [128, 128], BF16)
        Gn = sb("Gn", [128, 968], F32)
        cs_e = sb("cs_e", [128, 8, 2, 16], F32)
        cs_o = sb("cs_o", [128, 8, 2, 32], F32)
        cval = sb("cval", [128, 64], F32)
        swmb = [sb("swmb%d" % i, [128, 256], BF16) for i in range(1)]
        oidx = sb("oidx", [128, 2], mybir.dt.int32)
        cs_own = sb("cs_own", [128, 2, 2, 32], F32)
        gate_rep = [sb("gate_rep%d" % i, [128, D], F32) for i in range(2)]
        modc = sb("modc", [128, 16, 2], F32)
        bcol = sb("bcol", [128, 24], F32)
        gcol = sb("gcol", [128, 8], F32)
        ccol = sb("ccol", [128, 8, 2], F32)
        scT = sb("scT", [128, 8, 2], BF16)
        dadd = sb("dadd", [128, 16], F32)
        mhalf = sb("mhalf", [128, 16], F32)
        epsc = sb("epsc", [128, 2], F32)
        fence_t = sb("fence_t", [128, 2], F32)
        xr = sb("xr", [128, D], F32)
        tth = [sb("tth%d" % i, [128, 15, 64], BF16) for i in range(2)]
        PT = [sb("PT%d" % i, [128, 2, 256], BF16) for i in range(2)]
        Qbd = [sb("Qbd%d" % i, [128, 2, 256], BF16) for i in range(2)]
        qa_t = sb("qa_t", [128, 768], F32)
        qa_buf, qa_b = qa_t, qa_t.b
        smalls = {}

        def small(name, n):
            if name not in smalls:
                smalls[name] = sb("sm_" + name, [128, n], F32)
            return smalls[name]

        pz = [ps("pz%d" % i, [128, 512], F32) for i in range(2)]
        pt = [ps("pt%d" % i, [128, 1024], BF16) for i in range(2)]
        psc_t = [ps("psc%d" % i, [128, 2, 256], F32) for i in range(2)]
        psc_b = [[Buf("psc%d_%d" % (i, j)) for j in range(2)] for i in range(2)]
        po = [ps("po%d" % i, [128, 512], F32) for i in range(2)]

        rot = {}

        class PV:
            def __init__(self, i):
                self.ap = psc_t[i][:, :, :].rearrange("p a b -> p (a b)")
                self.b = psc_b[i][0]

            def __getitem__(self, k):
                return self.ap[k]

        pzs = None

        def pzbuf():
            sid = S.stream_id()
            if sid == 1:
                return pzs[nxt("pzs", 2)]
            if sid == 0:
                return pz[nxt("pz0", 2)]
            return pz[nxt("pz", 2)]

        pzs = [PV(0), PV(1)]

        def nxt(name, n):
            if name == "pt" and S.stream_id() is not None:
                return S.stream_id()
            v = rot.get(name, 0)
            rot[name] = (v + 1) % n
            return v

        def dve(fn, reads, writes):
            S.op("dve", fn, reads=reads, writes=writes)

        def pool(fn, reads, writes):
            S.op("pool", fn, reads=reads, writes=writes)

        def act(fn, reads, writes):
            S.op("act", fn, reads=reads, writes=writes)

        def pe(fn, reads, writes):
            S.op("pe", fn, reads=reads, writes=writes)

        def load(out, in_, owner, reads=(), extra_w=(), **kw):
            S.dma("sp", out, in_, owner, reads=list(reads), writes=[owner] + list(extra_w), **kw)

        def zalias(t):
            i = 0 if t is zb[0] else 1
            return zg[i]

        stg_b = [Buf("stg%d" % k) for k in range(4)]

        def stg_slot():
            k = nxt("stg", 4)
            return zb[k // 2], (k % 2) * 1024, stg_b[k]

        def z_fence():
            dve(lambda e: e.memset(fence_t[:], 0.0), [], [fence_t.b, zb[0].b, zb[1].b] + zg[0] + zg[1] + stg_b)

        def store(out, in_, owner, dst=(), final=False, **kw):
            S.dma("pool", out, in_, owner, reads=[owner], writes=list(dst), final=final, **kw)

        def transposes(srcs, src_bufs, dst_ap_fn, dst_buf, rows=128):
            p = pt[nxt("pt", 2)]
            n = len(srcs)
            for j, s_ap in enumerate(srcs):
                pe(lambda e, j=j, s_ap=s_ap, p=p: e.transpose(out=p[0:rows, j * 128:(j + 1) * 128], in_=s_ap, identity=ident[:]),
                   reads=list(src_bufs) + [ident.b], writes=[p.b])
            src_ps = p[0:rows, 0:n * 128].rearrange("p (j t) -> p j t", t=128)
            if nxt("tr_ev", 2) == 0:
                act(lambda e: e.activation(out=dst_ap_fn(), in_=src_ps, func=ACTF.Copy), reads=[p.b], writes=[dst_buf])
            else:
                dve(lambda e: e.tensor_copy(out=dst_ap_fn(), in_=src_ps), reads=[p.b], writes=[dst_buf])

        def rstd_of(src_ap, H, d, src_bufs, name, extra=None):
            ss = small("ss_" + name, H)
            rs = small("rs_" + name, H)
            sq = U1[:, 0:H * d].rearrange("p (h d) -> p h d", d=d)
            dve(lambda e: e.tensor_tensor(out=sq, in0=src_ap, in1=src_ap, op=ALU.mult), src_bufs, [U1.b])
            dve(lambda e: e.tensor_reduce(out=ss[:, 0:H], in_=sq, axis=AX.X, op=ALU.add), [U1.b], [ss.b])
            n = d
            if extra is not None:
                ex_ap, ex_buf, n = extra
                dve(lambda e: e.tensor_tensor(out=ss[:, 0:H], in0=ss[:, 0:H], in1=bc(ex_ap, [128, H]), op=ALU.add),
                    [ss.b, ex_buf], [ss.b])
            dve(lambda e: e.tensor_scalar(out=ss[:, 0:H], in0=ss[:, 0:H], scalar1=1.0 / n, scalar2=EPS, op0=ALU.mult, op1=ALU.add),
                reads=[ss.b], writes=[ss.b])
            pool(lambda e: e.tensor_tensor(out=rs[:, 0:H], in0=ss[:, 0:H], in1=mhalf[:, 0:H], op=ALU.pow),
                 reads=[ss.b, mhalf.b], writes=[rs.b])
            return rs

        def norm_heads(src_ap, H, d, src_bufs, gain_ap, out_ap, out_bufs, name, rs=None):
            if rs is None:
                rs = rstd_of(src_ap, H, d, src_bufs, name)
            tmp = U1[:, 0:H * d].rearrange("p (h d) -> p h d", d=d)
            dve(lambda e: e.tensor_tensor(out=tmp, in0=src_ap, in1=bc(rs[:, 0:H].unsqueeze(2), [128, H, d]), op=ALU.mult),
                reads=list(src_bufs) + [rs.b], writes=[U1.b])
            dve(lambda e: e.tensor_tensor(out=out_ap, in0=tmp, in1=bc(gain_ap.unsqueeze(1), [128, H, d]), op=ALU.mult),
                reads=[U1.b, Gn.b], writes=out_bufs)

        def rope(x_ap, H, half, cs_tile, ti, bufs):
            cos = bc(cs_tile[:, ti, 0, :].unsqueeze(1), [128, H, half])
            sin = bc(cs_tile[:, ti, 1, :].unsqueeze(1), [128, H, half])
            x1 = x_ap[:, :, 0:half]
            x2 = x_ap[:, :, half:2 * half]
            n = H * half
            ta = U2[:, 0:n].rearrange("p (h d) -> p h d", d=half)
            tb_ = U2[:, n:2 * n].rearrange("p (h d) -> p h d", d=half)
            tc = U2[:, 2 * n:3 * n].rearrange("p (h d) -> p h d", d=half)
            td = U2[:, 3 * n:4 * n].rearrange("p (h d) -> p h d", d=half)
            rb = list(bufs) + [cs_tile.b]
            dve(lambda e: e.tensor_tensor(out=ta, in0=x1, in1=cos, op=ALU.mult), reads=rb, writes=[U2.b])
            dve(lambda e: e.tensor_tensor(out=tb_, in0=x2, in1=sin, op=ALU.mult), reads=rb, writes=[U2.b])
            dve(lambda e: e.tensor_tensor(out=tc, in0=x1, in1=sin, op=ALU.mult), reads=rb, writes=[U2.b])
            dve(lambda e: e.tensor_tensor(out=td, in0=x2, in1=cos, op=ALU.mult), reads=rb, writes=[U2.b])
            dve(lambda e: e.tensor_tensor(out=x1, in0=ta, in1=tb_, op=ALU.subtract), reads=[U2.b], writes=list(bufs))
            dve(lambda e: e.tensor_tensor(out=x2, in0=tc, in1=td, op=ALU.add), reads=[U2.b], writes=list(bufs))

        identf = xtmp
        for qq in Qbd:
            pool(lambda e, qq=qq: e.memset(qq[:], 0.0), [], [qq.b])
        pool(lambda e: e.memset(identf[:, 0:128], 0.0), [], [identf.b])
        pool(lambda e: e.affine_select(out=identf[:, 0:128], in_=identf[:, 0:128], pattern=[[-1, 128]], compare_op=ALU.not_equal,
                                       fill=1.0, base=0, channel_multiplier=1), [identf.b], [identf.b])
        dve(lambda e: e.tensor_copy(out=ident[:], in_=identf[:, 0:128]), [identf.b], [ident.b])
        pool(lambda e: e.memset(mhalf[:], -0.5), [], [mhalf.b])
        pool(lambda e: e.memset(epsc[:, 0:1], EPS), [], [epsc.b])
        pool(lambda e: e.memset(epsc[:, 1:2], 1.0), [], [epsc.b])
        pool(lambda e: e.memset(Vg[:, :, :, 64:65], 1.0), [], [Vg.b])
        load(Gn[:], gains[0:1, :].to_broadcast([128, 968]), Gn.b)
        load(cs_e[:], rope_e.rearrange("(j p) a h -> p j a h", p=128), cs_e.b)
        load(cs_o[:], rope_o.rearrange("(j p) a h -> p j a h", p=128), cs_o.b)
        load(cval[:], colvalid[:, :], cval.b)
        load(oidx[:], own_idx.rearrange("(j p) o -> p (j o)", p=128), oidx.b, allow_slow_non_contiguous=True)
        load(cs_own[:], rope_own.rearrange("(j p) a h -> p j a h", p=128), cs_own.b)
        dve(lambda e: e.tensor_scalar(out=Gn[:, 384:480], in0=Gn[:, 384:480], scalar1=96.0 ** -0.5, scalar2=None, op0=ALU.mult), [Gn.b], [Gn.b])
        for lo in (576, 704, 832):
            dve(lambda e, lo=lo: e.tensor_scalar(out=Gn[:, lo:lo + 64], in0=Gn[:, lo:lo + 64], scalar1=0.125, scalar2=None, op0=ALU.mult), [Gn.b], [Gn.b])
        act(lambda e: e.activation(out=Gn[:, 960:968], in_=Gn[:, 960:968], func=ACTF.Exp), [Gn.b], [Gn.b])
        G_qa, G_kva = Gn[:, 0:256], Gn[:, 256:384]
        G_q, G_k = Gn[:, 384:480], Gn[:, 480:576]
        G_naq, G_nak = Gn[:, 576:640], Gn[:, 640:704]
        G_gq, G_gk = Gn[:, 704:768], Gn[:, 768:832]
        G_sq, G_sk = Gn[:, 832:896], Gn[:, 896:960]

        for h in range(8):
            tf = zb[h % 2]
            for half in range(2):
                load(tf[half * 64:(half + 1) * 64, 0:960], rpbT[:, h, :], tf.b, extra_w=zalias(tf))
            act(lambda e, tf=tf: e.activation(out=tf[:, 0:960], in_=tf[:, 0:960], func=ACTF.Exp), [tf.b], [tf.b])
            tt = tth[h % 2]
            dve(lambda e, tf=tf, tt=tt: e.tensor_tensor(out=tt[:], in0=tf[:, 0:960].rearrange("p (r c) -> p r c", c=64),
                                                    in1=bc(cval[:].unsqueeze(1), [128, 15, 64]), op=ALU.mult),
                [tf.b, cval.b], [tt.b])
            store(ttd[:, h, :], tt[:].rearrange("p r c -> p (r c)"), tt.b, dst=[ttd_b[h]])
        z_fence()

        def setup_mod(l):
            z_fence()
            load(bcol[:], b_mod[l].rearrange("(j p) -> p j", p=128), bcol.b, allow_slow_non_contiguous=True)
            load(gcol[:], norm_g[l].rearrange("(j p) -> p j", p=128), gcol.b, allow_slow_non_contiguous=True)
            brep = U2
            scRv = Ob[:, :, :].rearrange("p q n -> p (q n)").rearrange("p (k a t) -> p k a t", k=8, a=2)
            if True:
                load(ccol[:, :, 0], c_ctx.rearrange("(j p) -> p j", p=128), ccol.b, allow_slow_non_contiguous=True)
                load(ccol[:, :, 1], c_s.rearrange("(j p) -> p j", p=128), ccol.b, allow_slow_non_contiguous=True)
                t = small("silu", 16)
                tv = t[:, 0:16].rearrange("p (j a) -> p j a", a=2)
                act(lambda e: e.activation(out=tv, in_=ccol[:], func=ACTF.Exp, scale=-1.0), [ccol.b], [t.b])
                act(lambda e: e.activation(out=tv, in_=tv, func=ACTF.Ln, scale=1.0, bias=epsc[:, 1:2]), [t.b, epsc.b], [t.b])
                act(lambda e: e.activation(out=tv, in_=tv, func=ACTF.Exp, scale=-1.0), [t.b], [t.b])
                dve(lambda e: e.tensor_tensor(out=scT[:], in0=tv, in1=ccol[:], op=ALU.mult), [t.b, ccol.b], [scT.b])
                dve(lambda e: e.tensor_copy(out=scRv, in_=bc(scT[:].unsqueeze(3), [128, 8, 2, 128])), [scT.b], [Ob.b])
            load(brep[:], b_mod[l:l + 1, 2048:3072].to_broadcast([128, 1024]), brep.b)
            pm = psc_t[0]
            gacc = [[pz[0], pz[1]], [po[0], po[1]]]
            for kc in range(8):
                for piece in range(3):
                    stg, so, sbuf_ = stg_slot()
                    load(stg[:, so:so + 1024], w_mod[l, kc * 128:(kc + 1) * 128, piece * 1024:(piece + 1) * 1024], sbuf_)
                    wm = tb if nxt("wmrr", 2) == 0 else xsb
                    if nxt("castrr2", 2) == 0:
                        act(lambda e, stg=stg, so=so, wm=wm: e.activation(out=wm[:], in_=stg[:, so:so + 1024], func=ACTF.Copy), [sbuf_], [wm.b])
                    else:
                        dve(lambda e, stg=stg, so=so, wm=wm: e.tensor_copy(out=wm[:], in_=stg[:, so:so + 1024]), [sbuf_], [wm.b])
                    if piece < 2:
                        for j in range(8):
                            nch = piece * 8 + j
                            pe(lambda e, j=j, nch=nch, kc=kc, wm=wm: e.matmul(out=pm[:, 0, nch * 2:nch * 2 + 2], lhsT=wm[:, j * 128:(j + 1) * 128],
                                                                       rhs=scT[:, kc, :], start=(kc == 0 and nch == 0), stop=(kc == 7 and nch == 15),
                                                                       skip_group_check=True),
                               [wm.b, scT.b], [psc_b[0][0]])
                    else:
                        for cond in range(2):
                            for j in range(2):
                                g = gacc[cond][j]
                                pe(lambda e, g=g, j=j, cond=cond, kc=kc, wm=wm: e.matmul(out=g[:], lhsT=scRv[:, kc, cond, :], rhs=wm[:, j * 512:(j + 1) * 512],
                                                                                  start=(kc == 0), stop=(kc == 7)),
                                   [wm.b, Ob.b], [g.b])
            pmv = pm[:, 0, 0:32].rearrange("p (n a) -> p n a", a=2)
            dve(lambda e: e.tensor_tensor(out=modc[:], in0=pmv, in1=bc(bcol[:, 0:16].unsqueeze(2), [128, 16, 2]), op=ALU.add),
                [psc_b[0][0], bcol.b], [modc.b])
            dve(lambda e: e.scalar_tensor_tensor(out=modc[:, 8:16, :], in0=modc[:, 8:16, :], scalar=1.0, in1=bc(gcol[:].unsqueeze(2), [128, 8, 2]),
                                                 op0=ALU.add, op1=ALU.mult), [modc.b, gcol.b], [modc.b])
            for cond in range(2):
                for j in range(2):
                    g = gacc[cond][j]
                    dve(lambda e, g=g, cond=cond, j=j: e.tensor_tensor(out=gate_rep[cond][:, j * 512:(j + 1) * 512], in0=g[:],
                                                                       in1=brep[:, j * 512:(j + 1) * 512], op=ALU.add),
                        [g.b, brep.b], [gate_rep[cond].b])

        def load_cast(dst_ap_fn, src_ap, ncols, dst_buf):
            stg, so, sbuf_ = stg_slot()
            load(stg[:, so:so + ncols], src_ap, sbuf_)
            r = nxt("castrr", 5)
            if r in (0, 2):
                act(lambda e: e.activation(out=dst_ap_fn(), in_=stg[:, so:so + ncols], func=ACTF.Copy), [sbuf_], [dst_buf])
            elif r in (1, 3):
                dve(lambda e: e.tensor_copy(out=dst_ap_fn(), in_=stg[:, so:so + ncols]), [sbuf_], [dst_buf])
            else:
                pool(lambda e: e.tensor_copy(out=dst_ap_fn(), in_=stg[:, so:so + ncols]), [sbuf_], [dst_buf])

        def setup_weights(l):
            if l == 0:
                win_d, ncol, wout_d = w_in_e, EV_IN, w_out_e
            else:
                win_d, ncol, wout_d = w_in_o, OD_IN, w_out_o
            pieces = [(0, 992), (992, 992), (1984, 992)] if l == 0 else [(0, 1024), (1024, 1024), (2048, 512)]
            for kc in range(8):
                for (c0, cn) in pieces:
                    load_cast(lambda kc=kc, c0=c0, cn=cn: w_in[:, kc, c0:c0 + cn],
                              win_d[kc * 128:(kc + 1) * 128, c0:c0 + cn], cn, w_in_b[kc])
            if l == 0:
                for j in range(2):
                    load_cast(lambda j=j: wq[:, j, :], w_q_up[j * 128:(j + 1) * 128, :], 768, wq.b)
                load_cast(lambda: wkv[:], w_kv_up[:, :], 1024, wkv.b)
            for kc in range(8):
                load_cast(lambda kc=kc: w_out[:, kc, :], wout_d[kc * 128:(kc + 1) * 128, :], 1024, w_out_b[kc])
            pool(lambda e: e.memset(dadd[:], 0.0), [], [dadd.b])
            if l == 1:
                pool(lambda e: e.tensor_copy(out=dadd[:, 8:16], in_=Gn[:, 960:968]), [Gn.b], [dadd.b])
            z_fence()

        def gather_x(dst_t, col, src_bufs):
            S.dma_custom("pool", lambda e: e.indirect_dma_start(out=dst_t[:], out_offset=None, in_=x1s[:, :],
                                                                in_offset=bass.IndirectOffsetOnAxis(ap=oidx[:, col:col + 1], axis=0)),
                         dst_t.b, reads=list(src_bufs) + [oidx.b], writes=[dst_t.b])

        def load_x(x_src_ap, src_bufs, gather_col=None):
            xt = xb[nxt("xb", 2)]
            if gather_col is None:
                load(xt[:], x_src_ap, xt.b, reads=src_bufs)
            else:
                gather_x(xt, gather_col, src_bufs)
            return xt

        def make_hT(xt, cond, slot):
            ss = small("ss_x", 1)
            rs = small("rs_x", 1)
            act(lambda e: e.activation(out=xsb[:], in_=xt[:], func=ACTF.Square, accum_out=ss[:, 0:1]), [xt.b], [xsb.b, ss.b])
            act(lambda e: e.activation(out=ss[:, 0:1], in_=ss[:, 0:1], func=ACTF.Ln, scale=1.0 / D, bias=epsc[:, 0:1]), [ss.b, epsc.b], [ss.b])
            act(lambda e: e.activation(out=rs[:, 0:1], in_=ss[:, 0:1], func=ACTF.Exp, scale=-0.5), [ss.b], [rs.b])
            dve(lambda e: e.tensor_scalar(out=xsb[:], in0=xt[:], scalar1=rs[:, 0:1], scalar2=None, op0=ALU.mult), [xt.b, rs.b], [xsb.b])
            p = pt[nxt("pt", 2)]
            for kc in range(8):
                pe(lambda e, kc=kc, p=p: e.transpose(out=p[:, kc * 128:(kc + 1) * 128], in_=xsb[:, kc * 128:(kc + 1) * 128], identity=ident[:]),
                   [xsb.b, ident.b], [p.b])
            for kc in range(8):
                act(lambda e, kc=kc, p=p: e.activation(out=hTb[:, kc, slot * 128:(slot + 1) * 128], in_=p[:, kc * 128:(kc + 1) * 128],
                                                       func=ACTF.Identity, scale=modc[:, 8 + kc, cond:cond + 1], bias=modc[:, kc, cond:cond + 1]),
                    [p.b, modc.b], [hTb.b])

        def inproj(slot, groups, z, zgb):
            for (rhs_fn, N, zoff, gi) in groups:
                p = pzbuf()
                for kc in range(8):
                    pe(lambda e, kc=kc, p=p, rhs_fn=rhs_fn, N=N: e.matmul(out=p[:, 0:N], lhsT=hTb[:, kc, slot * 128:(slot + 1) * 128],
                                                                        rhs=rhs_fn(kc), start=(kc == 0), stop=(kc == 7)),
                       [hTb.b, w_in_b[kc]], [p.b])
                if nxt("ip_ev", 2) == 0:
                    act(lambda e, p=p, N=N, zoff=zoff: e.activation(out=z[:, zoff:zoff + N], in_=p[:, 0:N], func=ACTF.Copy), [p.b], [zgb[gi]])
                else:
                    dve(lambda e, p=p, N=N, zoff=zoff: e.tensor_copy(out=z[:, zoff:zoff + N], in_=p[:, 0:N]), [p.b], [zgb[gi]])

        def wcol(lo, n):
            return lambda kc: w_in[:, kc, lo:lo + n]

        def even_kv_post(z, zgb, chunk, ti, is_sample, is_ctx, out_seq=None, out_tile=None):
            A, B, C = zgb[0], zgb[1], zgb[2]
            kb3 = z[:, 512:1024].rearrange("p (h d) -> p h d", d=64)
            rs_kb = None
            if not is_ctx:
                rs_kb = rstd_of(kb3, 8, 64, [B], "kb")
            ssr = small("ssr", 1)
            junk = U1[:, 0:32]
            dve(lambda e: e.tensor_tensor(out=junk, in0=z[:, 128:160], in1=z[:, 128:160], op=ALU.mult), [A], [U1.b])
            dve(lambda e: e.tensor_reduce(out=ssr[:, 0:1], in_=junk, axis=AX.X, op=ALU.add), [U1.b], [ssr.b])
            ckv = z[:, 0:128]
            if not is_ctx:
                norm_heads(z[:, 0:128].rearrange("p (h d) -> p h d", d=128), 1, 128, [A], G_kva,
                           z[:, 0:128].rearrange("p (h d) -> p h d", d=128), [A], "ckv")
                if out_seq is not None:
                    store(n_ckv[out_seq, out_tile * 128:(out_tile + 1) * 128, :], z[:, 0:128], A, final=True)
                    store(n_krope[out_seq, out_tile * 128:(out_tile + 1) * 128, :], z[:, 128:160], A, final=True)
            dve(lambda e: e.tensor_copy(out=tb2[:, 0:128], in_=ckv), [A], [tb2.b])
            transposes([tb2[:, 0:128]], [tb2.b], lambda: ckvT[:].unsqueeze(1), ckvT.b)
            for j in range(2):
                p = pzbuf()
                pe(lambda e, p=p, j=j: e.matmul(out=p[:], lhsT=ckvT[:], rhs=wkv[:, j * 512:(j + 1) * 512], start=True, stop=True),
                   [ckvT.b, wkv.b], [p.b])
                act(lambda e, p=p, j=j: e.activation(out=U2[:, j * 512:(j + 1) * 512], in_=p[:], func=ACTF.Copy), [p.b], [U2.b])
            kv = U2[:, :].rearrange("p (h d) -> p h d", d=128)
            if not is_ctx:
                norm_heads(kb3, 8, 64, [B], G_nak, kb3, [B], "kb", rs=rs_kb)
                if out_seq is not None:
                    store(n_nak[out_seq, :, out_tile * 128:(out_tile + 1) * 128, :].rearrange("h t d -> t h d"), kb3, B, final=True)
                    store(n_nav[out_seq, :, out_tile * 128:(out_tile + 1) * 128, :].rearrange("h t d -> t h d"),
                          z[:, 1024:1536].rearrange("p (h d) -> p h d", d=64), C, final=True)
            dve(lambda e: e.tensor_copy(out=tb[:, 512:1024], in_=z[:, 512:1024]), [B], [tb.b])
            transposes([tb[:, 512 + j * 128:512 + (j + 1) * 128] for j in range(4)], [tb.b],
                       lambda: KbT[:, :, chunk * 128:(chunk + 1) * 128], KbT.b)
            act(lambda e: e.activation(out=Vg[:, chunk, 8:16, 0:64], in_=z[:, 1024:1536].rearrange("p (h d) -> p h d", d=64), func=ACTF.Copy), [C], [Vg.b])
            dve(lambda e: e.tensor_copy(out=Vg[:, chunk, 0:8, 0:64], in_=kv[:, :, 64:128]), [U2.b], [Vg.b])
            rs = rstd_of(kv[:, :, 0:64], 8, 64, [U2.b], "ka", extra=(ssr[:, 0:1], ssr.b, 96))
            dve(lambda e: e.tensor_tensor(out=tb[:, 0:512].rearrange("p (h d) -> p h d", d=64), in0=kv[:, :, 0:64],
                                          in1=bc(G_k[:, 0:64].unsqueeze(1), [128, 8, 64]), op=ALU.mult), [U2.b, Gn.b], [tb.b])
            transposes([tb[:, j * 128:(j + 1) * 128] for j in range(4)], [tb.b],
                       lambda: KnT[:, :, chunk * 128:(chunk + 1) * 128], KnT.b)
            dve(lambda e: e.tensor_copy(out=rsk[:, chunk, :], in_=rs[:, 0:8]), [rs.b], [rsk.b])
            kr = small("kr", 32)
            dve(lambda e: e.tensor_tensor(out=kr[:, 0:32], in0=z[:, 128:160], in1=G_k[:, 64:96], op=ALU.mult), [A, Gn.b], [kr.b])
            if is_sample and not is_ctx:
                rope(kr[:, 0:32].rearrange("p (h d) -> p h d", h=1), 1, 16, cs_e, ti, [kr.b])
            dve(lambda e: e.tensor_copy(out=tb2[:, 128:160], in_=kr[:, 0:32]), [kr.b], [tb2.b])
            transposes([tb2[:, 128:160]], [tb2.b], lambda: KrT[:, chunk * 128:(chunk + 1) * 128].unsqueeze(1), KrT.b, rows=32)

        EV_KV_GROUPS = [(wcol(256, 160), 160, 0, 0), (wcol(1440, 512), 512, 512, 1), (wcol(1952, 512), 512, 1024, 2)]
        EV_Q_GROUPS = [(wcol(0, 256), 256, 0, 0), (wcol(416, 512), 512, 512, 1), (wcol(928, 512), 512, 1024, 2), (wcol(2464, 512), 512, 1536, 3)]

        def gates(z, zgb, gidx_off, qt):
            for n, (gi, zoff) in enumerate(gidx_off):
                t = U2[:, n * 512:(n + 1) * 512]
                act(lambda e, t=t, zoff=zoff: e.activation(out=t, in_=z[:, zoff:zoff + 512], func=ACTF.Exp, scale=-1.0), [zgb[gi]], [U2.b])
                act(lambda e, t=t: e.activation(out=t, in_=t, func=ACTF.Ln, scale=1.0, bias=epsc[:, 1:2]), [U2.b, epsc.b], [U2.b])
                act(lambda e, t=t: e.activation(out=t, in_=t, func=ACTF.Exp, scale=-1.0), [U2.b], [U2.b])
                dve(lambda e, t=t, zoff=zoff, n=n: e.tensor_tensor(out=Gb[:, qt, n * 512:(n + 1) * 512], in0=t, in1=z[:, zoff:zoff + 512], op=ALU.mult),
                    [U2.b, zgb[gi]], [Gb.b])

        def even_q_post(z, zgb, qt, ti, is_sample):
            rs_qb = rstd_of(z[:, 1024:1536].rearrange("p (h d) -> p h d", d=64), 8, 64, [zgb[2]], "qb")
            ql3 = z[:, 0:256].rearrange("p (h d) -> p h d", d=256)
            norm_heads(ql3, 1, 256, [zgb[0]], G_qa, tb2[:, 0:256].rearrange("p (h d) -> p h d", d=256), [tb2.b], "ql")
            transposes([tb2[:, 0:128], tb2[:, 128:256]], [tb2.b], lambda: qlT[:], qlT.b)
            for n in range(2):
                p = pzbuf()
                for j in range(2):
                    pe(lambda e, p=p, j=j, n=n: e.matmul(out=p[:, 0:384], lhsT=qlT[:, j, :], rhs=wq[:, j, n * 384:(n + 1) * 384],
                                                       start=(j == 0), stop=(j == 1)), [qlT.b, wq.b], [p.b])
                act(lambda e, p=p, n=n: e.activation(out=qa_buf[:, n * 384:(n + 1) * 384], in_=p[:, 0:384], func=ACTF.Copy), [p.b], [qa_b])
            q3 = qa_buf[:, 0:768].rearrange("p (h d) -> p h d", d=96)
            rs_qa = rstd_of(q3, 8, 96, [qa_b], "qa")
            qb3 = z[:, 1024:1536].rearrange("p (h d) -> p h d", d=64)
            norm_heads(qb3, 8, 64, [zgb[2]], G_naq, tb[:, 512:1024].rearrange("p (h d) -> p h d", d=64), [tb.b], "qb", rs=rs_qb)
            transposes([tb[:, 512 + j * 128:512 + (j + 1) * 128] for j in range(4)], [tb.b],
                       lambda: QbT[:, :, qt * 128:(qt + 1) * 128], QbT.b)
            norm_heads(q3, 8, 96, [qa_b], G_q, q3, [qa_b], "qa", rs=rs_qa)
            if is_sample:
                rope(q3[:, :, 64:96], 8, 16, cs_e, ti, [qa_b])
            dve(lambda e: e.tensor_copy(out=tb[:, 0:512].rearrange("p (h d) -> p h d", d=64), in_=q3[:, :, 0:64]), [qa_b], [tb.b])
            transposes([tb[:, j * 128:(j + 1) * 128] for j in range(4)], [tb.b],
                       lambda: QnT[:, :, qt * 128:(qt + 1) * 128], QnT.b)
            dve(lambda e: e.tensor_copy(out=tb2[:, 0:256].rearrange("p (h d) -> p h d", d=32), in_=q3[:, :, 64:96]), [qa_b], [tb2.b])
            transposes([tb2[:, h * 32:(h + 1) * 32] for h in range(8)], [tb2.b],
                       lambda: QrT[:, :, qt * 128:(qt + 1) * 128], QrT.b, rows=32)
            gates(z, zgb, [(1, 512), (3, 1536)], qt)


        def odd_kv_rhs(kc):
            return seg2(w_in[:, kc, 512:768], 1280, 2)

        OD_KV_GROUPS = [(odd_kv_rhs, 512, 0, 0)]
        OD_Q_GROUPS = [(wcol(0, 512), 512, 0, 0), (wcol(768, 512), 512, 512, 1), (wcol(1280, 512), 512, 1024, 2), (wcol(2048, 512), 512, 1536, 3)]

        def inproj_odd_kv(slot, z, zgb):
            p = pzbuf()
            for kc in range(8):
                pe(lambda e, kc=kc, p=p: e.matmul(out=p[:, 0:512].rearrange("p (a b) -> p a b", b=256), lhsT=hTb[:, kc, slot * 128:(slot + 1) * 128],
                                                 rhs=odd_kv_rhs(kc), start=(kc == 0), stop=(kc == 7)), [hTb.b, w_in_b[kc]], [p.b])
            act(lambda e, p=p: e.activation(out=z[:, 0:512], in_=p[:, 0:512], func=ACTF.Copy), [p.b], [zgb[0]])

        def odd_kv_post(z, zgb, chunk, ti, is_sample, is_ctx, out_seq=None, out_tile=None):
            A = zgb[0]
            for gi, (koff, voff, gain, kslot, vslot, ok, ov) in enumerate(((0, 128, G_gk, 0, 0, n_gk, n_gv), (256, 384, G_sk, 2, 2, n_sk, n_sv))):
                k3 = z[:, koff:koff + 128].rearrange("p (h d) -> p h d", d=64)
                v3 = z[:, voff:voff + 128].rearrange("p (h d) -> p h d", d=64)
                if not is_ctx:
                    norm_heads(k3, 2, 64, [A], gain, k3, [A], "ko%d" % gi)
                    if out_seq is not None:
                        store(ok[out_seq, :, out_tile * 128:(out_tile + 1) * 128, :].rearrange("h t d -> t h d"), k3, A, final=True)
                        store(ov[out_seq, :, out_tile * 128:(out_tile + 1) * 128, :].rearrange("h t d -> t h d"), v3, A, final=True)
                    if is_sample:
                        kr = small("kro%d" % gi, 128)
                        kr3 = kr[:, 0:128].rearrange("p (h d) -> p h d", d=64)
                        dve(lambda e, kr3=kr3, k3=k3: e.tensor_copy(out=kr3, in_=k3), [A], [kr.b])
                        rope(kr3, 2, 32, cs_o, ti, [kr.b])
                        ksrc, ksb = kr3, kr.b
                    else:
                        ksrc, ksb = k3, A
                else:
                    ksrc, ksb = k3, A
                o4 = tb2[:, 0:256].rearrange("p (h r d) -> p h r d", r=2, d=64)
                dve(lambda e, o4=o4, ksrc=ksrc: e.tensor_copy(out=o4, in_=bc(ksrc.unsqueeze(2), [128, 2, 2, 64])), [ksb], [tb2.b])
                transposes([tb2[:, 0:128], tb2[:, 128:256]], [tb2.b],
                           lambda kslot=kslot: KnT[:, kslot:kslot + 2, chunk * 128:(chunk + 1) * 128], KnT.b)
                pool(lambda e, v3=v3, vslot=vslot: e.tensor_copy(out=Vg[:, chunk, vslot:vslot + 2, 0:64], in_=v3), [A], [Vg.b])

        def odd_q_post(z, zgb, qt, ti, is_sample, cs=None):
            cs_t, cs_i = (cs_o, ti) if cs is None else cs
            rs_pre = [rstd_of(z[:, off:off + 512].rearrange("p (h d) -> p h d", d=64), 8, 64, [zgb[0] if gi == 0 else zgb[2]], "qo%d" % gi)
                      for gi, off in enumerate((0, 1024))]
            for gi, (off, gain, dstT) in enumerate(((0, G_gq, QnT), (1024, G_sq, QbT))):
                q3 = z[:, off:off + 512].rearrange("p (h d) -> p h d", d=64)
                zb_ = zgb[0] if gi == 0 else zgb[2]
                if is_sample:
                    norm_heads(q3, 8, 64, [zb_], gain, q3, [zb_], "qo%d" % gi, rs=rs_pre[gi])
                    rope(q3, 8, 32, cs_t, cs_i, [zb_])
                    dve(lambda e, q3=q3: e.tensor_copy(out=tb[:, 0:512].rearrange("p (h d) -> p h d", d=64), in_=q3), [zb_], [tb.b])
                else:
                    norm_heads(q3, 8, 64, [zb_], gain, tb[:, 0:512].rearrange("p (h d) -> p h d", d=64), [tb.b], "qo%d" % gi, rs=rs_pre[gi])
                transposes([tb[:, j * 128:(j + 1) * 128] for j in range(4)], [tb.b],
                           lambda dstT=dstT: dstT[:, :, qt * 128:(qt + 1) * 128], dstT.b)
            gates(z, zgb, [(1, 512), (3, 1536)], qt)

        def attention(l, is_sample, qb_idx, own=False):
            nlat = 8 if is_sample else 2
            ctx_chunks = [8, 9] if is_sample else []
            R = 4 * qb_idx

            def lo(r):
                return min(max(r - 4, 0), 8)

            items = []
            for pr in range(8):
                h0 = 2 * pr
                if l == 0:
                    if h0 < 8:
                        kind, chunks = "mla", list(range(nlat)) + ctx_chunks
                    else:
                        kind = "na"
                        if is_sample:
                            lat = {0: [0, 1, 2, 3], 1: [0, 1, 2, 3, 4, 5], 2: [2, 3, 4, 5, 6, 7], 3: [4, 5, 6, 7]}[qb_idx]
                        else:
                            lat = [0, 1]
                        chunks = lat + ctx_chunks
                else:
                    if h0 < 8:
                        kind, chunks = "gqa", list(range(nlat)) + ctx_chunks
                    else:
                        kind = "swa"
                        if own:
                            lat = list(range(8))
                        elif is_sample:
                            lat = [j for j in (2 * qb_idx - 1, 2 * qb_idx, 2 * qb_idx + 1, 2 * qb_idx + 2) if 0 <= j <= 7]
                        else:
                            lat = [0, 1]
                        chunks = lat + ctx_chunks
                for ci, j in enumerate(chunks):
                    items.append(dict(pr=pr, kind=kind, j=j, ci=ci, nch=len(chunks)))
            pair_tt = {}
            pair_q = {}

            def front(it):
                pr, kind, j = it["pr"], it["kind"], it["j"]
                h0 = 2 * pr
                if kind == "na" and is_sample and it["ci"] == 0:
                    tts = []
                    for hh in range(2):
                        tt = tth[hh]
                        load(tt[:].rearrange("p r c -> p (r c)"), ttd[:, h0 + hh - 8, :], tt.b, reads=[ttd_b[h0 + hh - 8]])
                        tts.append(tt)
                    pair_tt[pr] = tts
                bi = nxt("psc", 2)
                pst, psb = psc_t[bi], psc_b[bi][0]
                ks = slice(j * 128, (j + 1) * 128)
                if it["ci"] == 0:
                    qbd = Qbd[nxt("qbd", 2)]
                    qsrc = QnT if kind in ("mla", "gqa") else QbT
                    qpi = pr if kind in ("mla", "gqa") else pr - 4
                    dve(lambda e: e.tensor_copy(out=qbd[0:64, 0, :], in_=qsrc[0:64, qpi, :]), [qsrc.b], [qbd.b])
                    dve(lambda e: e.tensor_copy(out=qbd[64:128, 1, :], in_=qsrc[64:128, qpi, :]), [qsrc.b], [qbd.b])
                    pair_q[pr] = qbd
                qbd = pair_q[pr]
                if kind == "mla":
                    ksrc, kpi = KnT, pr
                elif kind == "na":
                    ksrc, kpi = KbT, pr - 4
                elif kind == "gqa":
                    ksrc, kpi = KnT, h0 // 4
                else:
                    ksrc, kpi = KnT, 2 + (h0 - 8) // 4
                pe(lambda e: e.matmul(out=pst[:], lhsT=ksrc[:, kpi, ks], rhs=qbd[:], start=True, stop=(kind != "mla")),
                   [ksrc.b, qbd.b], [psb])
                if kind == "mla":
                    pe(lambda e: e.matmul(out=pst[:], lhsT=KrT[:, ks], rhs=QrT[:, h0:h0 + 2, :], start=False, stop=True), [KrT.b, QrT.b], [psb])
                P = PT[nxt("PT", 2)]
                if kind == "mla":
                    for hh in range(2):
                        act(lambda e, hh=hh: e.activation(out=P[:, hh, :], in_=pst[:, hh, :], func=ACTF.Exp, scale=rsk[:, j, h0 + hh:h0 + hh + 1]),
                            [psb, rsk.b], [P.b])
                else:
                    act(lambda e: e.activation(out=P[:], in_=pst[:], func=ACTF.Exp), [psb], [P.b])
                if is_sample and j < 8:
                    if kind == "na":
                        for hh in range(2):
                            tt = pair_tt[pr][hh]
                            for a in range(2):
                                rp = 2 * j + a
                                val = [lo(R + i) <= rp < lo(R + i) + 8 for i in range(4)]
                                vi = [i for i in range(4) if val[i]]
                                psl = slice(a * 64, (a + 1) * 64)
                                if vi:
                                    i0, n = vi[0], len(vi)
                                    d0 = 7 - rp + R + i0
                                    eng = dve
                                    eng(lambda e, psl=psl, i0=i0, n=n, d0=d0, hh=hh, tt=tt: e.tensor_tensor(
                                        out=P[psl, hh, i0 * 64:(i0 + n) * 64], in0=P[psl, hh, i0 * 64:(i0 + n) * 64],
                                        in1=tt[psl, d0:d0 + n, :].rearrange("p r c -> p (r c)"), op=ALU.mult), [P.b, tt.b], [P.b])
                                for i in range(4):
                                    if not val[i]:
                                        pool(lambda e, psl=psl, i=i, hh=hh: e.memset(P[psl, hh, i * 64:(i + 1) * 64], 0.0), [], [P.b])
                    elif kind == "swa":
                        mt = swmb[0]
                        load(mt[:], swm_own[j, :, :], mt.b)
                        dve(lambda e: e.tensor_tensor(out=P[:], in0=P[:], in1=bc(mt[:].unsqueeze(1), [128, 2, 256]), op=ALU.mult),
                             [P.b, mt.b], [P.b])
                it["P"] = P

            def back(it):
                pr, kind, j, ci, nch, P = it["pr"], it["kind"], it["j"], it["ci"], it["nch"], it["P"]
                h0 = 2 * pr
                pacc = po[pr % 2]
                pv4 = pacc[:, 0:260].rearrange("p (q h d) -> p q h d", q=2, h=2)
                for hh in range(2):
                    h = h0 + hh
                    if kind in ("mla", "na"):
                        vsl = h
                    elif kind == "gqa":
                        vsl = h // 4
                    else:
                        vsl = 2 + (h - 8) // 4
                    for q in range(2):
                        pe(lambda e, q=q, hh=hh, vsl=vsl: e.matmul(out=pv4[:, q, hh, :], lhsT=P[:, hh, q * 128:(q + 1) * 128], rhs=Vg[:, j, vsl, :],
                                                                  start=(ci == 0 and hh == 0 and q == 0), stop=(ci == nch - 1 and hh == 1 and q == 1),
                                                                  skip_group_check=True), [P.b, Vg.b], [pacc.b])
                if ci == nch - 1:
                    den = small("den", 4)
                    dv = den[:, 0:4].rearrange("p (q h) -> p q h", q=2)
                    dve(lambda e: e.tensor_tensor(out=dv, in0=pv4[:, :, :, 64], in1=bc(dadd[:, h0:h0 + 2].unsqueeze(1), [128, 2, 2]), op=ALU.add),
                        [pacc.b, dadd.b], [den.b])
                    dve(lambda e: e.reciprocal(out=dv, in_=dv), [den.b], [den.b])
                    ov = Ob[:, :, h0 * 64:(h0 + 2) * 64].rearrange("p q (h d) -> p q h d", d=64)
                    gv = Gb[:, :, h0 * 64:(h0 + 2) * 64].rearrange("p q (h d) -> p q h d", d=64)
                    o4 = xtmp[:, 0:256].rearrange("p (q h d) -> p q h d", q=2, h=2)
                    dve(lambda e: e.tensor_tensor(out=o4, in0=pv4[:, :, :, 0:64], in1=bc(dv.unsqueeze(3), [128, 2, 2, 64]), op=ALU.mult),
                        [pacc.b, den.b], [xtmp.b])
                    dve(lambda e: e.tensor_tensor(out=ov, in0=o4, in1=gv, op=ALU.mult), [xtmp.b, Gb.b], [Ob.b])

            LA = 1
            for idx in range(len(items) + LA):
                if idx < len(items):
                    front(items[idx])
                if idx - LA >= 0:
                    back(items[idx - LA])
            for q in range(2):
                transposes([Ob[:, q, kc * 128:(kc + 1) * 128] for kc in range(8)], [Ob.b],
                           lambda q=q: ogT[:, :, q * 128:(q + 1) * 128], ogT.b)

        def outproj_residual(cond, slot, x_src_ap, src_bufs, dst_ap, dst_bufs, final, gather_col=None):
            if gather_col is None:
                S.dma("pool", xr[:], x_src_ap, xr.b, reads=list(src_bufs), writes=[xr.b])
            else:
                gather_x(xr, gather_col, src_bufs)
            for n in range(2):
                p = pzbuf()
                for kc in range(8):
                    pe(lambda e, p=p, kc=kc, n=n: e.matmul(out=p[:], lhsT=ogT[:, kc, slot * 128:(slot + 1) * 128], rhs=w_out[:, kc, n * 512:(n + 1) * 512],
                                                         start=(kc == 0), stop=(kc == 7)), [ogT.b, w_out_b[kc]], [p.b])
                dve(lambda e, p=p, n=n: e.tensor_tensor(out=xtmp[:], in0=p[:], in1=gate_rep[cond][:, n * 512:(n + 1) * 512], op=ALU.mult),
                    [p.b, gate_rep[cond].b], [xtmp.b])
                dve(lambda e, n=n: e.tensor_tensor(out=xr[:, n * 512:(n + 1) * 512], in0=xr[:, n * 512:(n + 1) * 512], in1=xtmp[:], op=ALU.add),
                    [xtmp.b, xr.b], [xr.b])
            store(dst_ap, xr[:], xr.b, dst=dst_bufs, final=final)

        dbg_done = set()

        def dbg(name, tile_ap, buf, shape, dt=F32):
            if not DEBUG or name in dbg_done:
                return
            dbg_done.add(name)
            d = dout("dbg_" + name, shape, dt)
            S.dma("sp", d, tile_ap, buf, reads=[buf], final=True)

        def run_jobs(jobs):
            prev = None
            prev_tail = None
            if jobs[0][0] is not None:
                jobs[0][0]()
            for i, (pre, s1, s2, tail) in enumerate(jobs):
                if i + 1 < len(jobs) and jobs[i + 1][0] is not None:
                    jobs[i + 1][0]()
                if prev is None:
                    s1()
                elif os.environ.get('K_NOINT'):
                    s1()
                    prev()
                else:
                    S.interleave(s1, prev, 3, 2)
                if prev_tail is not None:
                    prev_tail()
                prev = s2
                prev_tail = tail
            if prev is not None:
                prev()
            if prev_tail is not None:
                prev_tail()

        for l in range(2):
            setup_mod(l)
            dbg("modc", modc[:].rearrange("p a b -> p (a b)"), modc.b, [128, 32])
            setup_weights(l)
            q_groups = EV_Q_GROUPS if l == 0 else OD_Q_GROUPS
            last = (l == 1)
            jobs = []
            seqs = [("p", s_) for s_ in range(NPS)] + [("s", 0)]
            for kind, s_ in seqs:
                is_sample = kind == "s"
                cond = 1 if is_sample else 0
                ntile = 8 if is_sample else 2
                if is_sample:
                    src = xs if l == 0 else x1s
                    srcb = (lambda t: []) if l == 0 else (lambda t: [x1s_b[t]])
                    dst = ys if last else x1s
                    dstb = (lambda t: []) if last else (lambda t: [x1s_b[t]])
                    base = 0
                else:
                    src = xp if l == 0 else x1p
                    srcb = (lambda t: []) if l == 0 else (lambda t: [x1p_b[t]])
                    dst = yp if last else x1p
                    dstb = (lambda t: []) if last else (lambda t: [x1p_b[t]])
                    base = s_ * 2
                for ti in range(ntile):
                    st_ = {}

                    def p1_pre(ti=ti, st_=st_, src=src, srcb=srcb, base=base):
                        gt = base + ti
                        st_["xt"] = load_x(src[gt * 128:(gt + 1) * 128, :], srcb(gt))

                    def p1_s1(ti=ti, st_=st_, src=src, srcb=srcb, base=base, cond=cond):
                        slot = nxt("hslot", 2)
                        make_hT(st_["xt"], cond, slot)
                        zi = nxt("z", 2)
                        st_["z"], st_["zgb"] = zb[zi], zg[zi]
                        if l == 0:
                            inproj(slot, EV_KV_GROUPS, zb[zi], zg[zi])
                        else:
                            inproj_odd_kv(slot, zb[zi], zg[zi])

                    def p1_s2(ti=ti, st_=st_, is_sample=is_sample, s_=s_):
                        if l == 0:
                            even_kv_post(st_["z"], st_["zgb"], ti, ti, is_sample, False, None if is_sample else s_, ti)
                        else:
                            odd_kv_post(st_["z"], st_["zgb"], ti, ti, is_sample, False, None if is_sample else s_, ti)

                    jobs.append((p1_pre, p1_s1, p1_s2, None))
                if is_sample:
                    for cc in range(2):
                        st_ = {}

                        def cx_s1(cc=cc, st_=st_):
                            zi = nxt("z", 2)
                            z, zgb = zb[zi], zg[zi]
                            st_["z"], st_["zgb"] = z, zgb
                            tsl = slice(cc * 128, (cc + 1) * 128)
                            if l == 0:
                                load(z[:, 0:128], ca_ckv[tsl, :], zgb[0])
                                load(z[:, 128:160], ca_krope[tsl, :], zgb[0])
                                load(z[:, 512:1024].rearrange("p (h d) -> p h d", d=64), ca_nak[:, tsl, :].rearrange("h t d -> t h d"), zgb[1])
                                load(z[:, 1024:1536].rearrange("p (h d) -> p h d", d=64), ca_nav[:, tsl, :].rearrange("h t d -> t h d"), zgb[2])
                            else:
                                load(z[:, 0:128].rearrange("p (h d) -> p h d", d=64), ca_gk[:, tsl, :].rearrange("h t d -> t h d"), zgb[0])
                                load(z[:, 128:256].rearrange("p (h d) -> p h d", d=64), ca_gv[:, tsl, :].rearrange("h t d -> t h d"), zgb[0])
                                load(z[:, 256:384].rearrange("p (h d) -> p h d", d=64), ca_sk[:, tsl, :].rearrange("h t d -> t h d"), zgb[0])
                                load(z[:, 384:512].rearrange("p (h d) -> p h d", d=64), ca_sv[:, tsl, :].rearrange("h t d -> t h d"), zgb[0])

                        def cx_s2(cc=cc, st_=st_):
                            if l == 0:
                                even_kv_post(st_["z"], st_["zgb"], 8 + cc, 0, True, True)
                            else:
                                odd_kv_post(st_["z"], st_["zgb"], 8 + cc, 0, True, True)

                        jobs.append((None, cx_s1, cx_s2, None))
                own = is_sample and l == 1
                allsrc = [b_ for t_ in range(8) for b_ in srcb(t_)] if own else None
                for qb_idx in range(1 if own else ntile // 2):
                    for qt in range(2):
                        st_ = {}

                        def p2_pre(qb_idx=qb_idx, qt=qt, st_=st_, src=src, srcb=srcb, base=base, own=own, allsrc=allsrc):
                            gt = base + qb_idx * 2 + qt
                            if own:
                                st_["xt"] = load_x(None, allsrc, gather_col=qt)
                            else:
                                st_["xt"] = load_x(src[gt * 128:(gt + 1) * 128, :], srcb(gt))

                        def p2_s1(qb_idx=qb_idx, qt=qt, st_=st_, src=src, srcb=srcb, base=base, cond=cond, own=own, allsrc=allsrc):
                            slot = nxt("hslot", 2)
                            st_["slot"] = slot
                            make_hT(st_["xt"], cond, slot)
                            zi = nxt("z", 2)
                            st_["z"], st_["zgb"] = zb[zi], zg[zi]
                            inproj(slot, q_groups, zb[zi], zg[zi])

                        def p2_s2(qb_idx=qb_idx, qt=qt, st_=st_, is_sample=is_sample, src=src, srcb=srcb, dst=dst, dstb=dstb, base=base, cond=cond,
                                  own=own, allsrc=allsrc):
                            ti = qb_idx * 2 + qt
                            if l == 0:
                                even_q_post(st_["z"], st_["zgb"], qt, ti, is_sample)
                            elif own:
                                odd_q_post(st_["z"], st_["zgb"], qt, ti, is_sample, cs=(cs_own, qt))
                            else:
                                odd_q_post(st_["z"], st_["zgb"], qt, ti, is_sample)
                            if qt == 1:
                                st_["tail"] = True

                        def p2_tail(qb_idx=qb_idx, qt=qt, st_=st_, is_sample=is_sample, src=src, srcb=srcb, dst=dst, dstb=dstb, base=base, cond=cond,
                                    own=own, allsrc=allsrc):
                            if qt == 1:
                                attention(l, is_sample, qb_idx, own=own)
                                for q2 in range(2):
                                    gt = base + qb_idx * 2 + q2
                                    if own:
                                        outproj_residual(cond, q2, None, allsrc, ys[q2 * 128:(q2 + 1) * 128, :], [], True, gather_col=q2)
                                    else:
                                        outproj_residual(cond, q2, src[gt * 128:(gt + 1) * 128, :], srcb(gt),
                                                         dst[gt * 128:(gt + 1) * 128, :], dstb(gt), last)

                        jobs.append((p2_pre, p2_s1, p2_s2, p2_tail if qt == 1 else None))
            run_jobs(jobs)
        S.emit()
    return nc


_NC_CACHE = {}


def _consts():
    c = np.arange(64)
    cs = np.clip(c - 8, 0, 48)
    cv = ((c[:, None] >= cs[None, :]) & (c[:, None] < cs[None, :] + 16)).astype(np.float32)
    colvalid = np.concatenate([cv, cv], axis=0)
    k = np.arange(128)[:, None]
    q = np.arange(256)[None, :]
    swamask = []
    for qi in range(4):
        swamask.append(np.stack([(np.abs((256 * qi + q) - (128 * j + k)) <= 128) for j in range(8)], axis=0)
                       .astype(np.float32).astype(ml_dtypes.bfloat16))

    def rope_tab(rot):
        quarter = rot // 4
        t = np.arange(SS)
        inv = (10000.0 ** (-np.arange(quarter, dtype=np.float32) / quarter)).astype(np.float32)
        row = (t // 64).astype(np.float32)[:, None] * inv
        col = (t % 64).astype(np.float32)[:, None] * inv
        ang = np.concatenate([row, col], axis=-1).astype(np.float32)
        return np.stack([np.cos(ang), np.sin(ang)], axis=1).astype(np.float32)

    return colvalid, swamask, rope_tab(32), rope_tab(64)


def _rpb_expand(rpb):
    cp = np.arange(64)[:, None]
    c = np.arange(64)[None, :]
    dc = np.clip(cp - c + 15, 0, 30)
    drp = 14 - np.arange(15)
    out = rpb[:, drp][:, :, dc]
    return np.ascontiguousarray(out.transpose(2, 0, 1, 3)).reshape(64, 8, 960)


def kernel(x_prompt, x_sample, cache_mla_ckv, cache_mla_krope, cache_na_k, cache_na_v, cache_gqa_k, cache_gqa_v,
           cache_swa_k, cache_swa_v, c, c_ctx, norm_g, w_mod, b_mod, w_in_even, mla_qa_g, w_q_up, mla_kva_g,
           w_kv_up, mla_q_g, mla_k_g, na_q_g, na_k_g, na_rpb, w_out_even, w_in_odd, gqa_q_g, gqa_k_g, swa_q_g,
           swa_k_g, swa_sink, w_out_odd):
    f = lambda a: np.ascontiguousarray(np.asarray(a), dtype=np.float32)
    if "nc" not in _NC_CACHE:
        _NC_CACHE["nc"] = build_program()
    nc = _NC_CACHE["nc"]
    colvalid, swamask, rope_e, rope_o = _consts()
    gains = np.concatenate([f(mla_qa_g)[0], f(mla_kva_g)[0], f(mla_q_g)[0], f(mla_k_g)[0], f(na_q_g)[0], f(na_k_g)[0],
                            f(gqa_q_g)[0], f(gqa_k_g)[0], f(swa_q_g)[0], f(swa_k_g)[0], f(swa_sink)[0]])[None, :]
    shared = {
        "c_ctx": f(c_ctx), "norm_g": f(norm_g), "w_mod": f(w_mod), "b_mod": f(b_mod),
        "w_in_e": f(w_in_even)[0], "w_q_up": f(w_q_up)[0], "w_kv_up": f(w_kv_up)[0], "w_out_e": f(w_out_even)[0],
        "w_in_o": f(w_in_odd)[0], "w_out_o": f(w_out_odd)[0], "gains": f(gains), "rpbT": f(_rpb_expand(f(na_rpb)[0])),
        "colvalid": colvalid, "rope_e": rope_e, "rope_o": rope_o,
    }
    xpf, xsf = f(x_prompt), f(x_sample)
    in_maps = []
    for core in range(NCORES):
        b = core // 4
        m = dict(shared)
        m["xp"] = xpf[core * NPS:(core + 1) * NPS].reshape(NPS * SEQ, D)
        qi = core % 4
        m["swm_own"] = swamask[qi]
        m["own_idx"] = (qi * 256 + np.arange(256, dtype=np.int32)).reshape(256, 1)
        m["rope_own"] = np.ascontiguousarray(rope_o[qi * 256:(qi + 1) * 256])
        m["xs"] = xsf[b]
        m["c_s"] = f(c)[b]
        m["ca_ckv"] = f(cache_mla_ckv)[b, 0]
        m["ca_krope"] = f(cache_mla_krope)[b, 0]
        m["ca_nak"] = f(cache_na_k)[b, 0]
        m["ca_nav"] = f(cache_na_v)[b, 0]
        m["ca_gk"] = f(cache_gqa_k)[b, 0]
        m["ca_gv"] = f(cache_gqa_v)[b, 0]
        m["ca_sk"] = f(cache_swa_k)[b, 0]
        m["ca_sv"] = f(cache_swa_v)[b, 0]
        in_maps.append(m)
    res = run_bass_kernel_spmd(nc, in_maps, core_ids=list(range(NCORES)))
    r = res.results
    cat = lambda name: np.concatenate([r[i][name] for i in range(NCORES)], axis=0)
    y_prompt = cat("yp").reshape(32, SEQ, D)
    y_sample = np.stack([np.concatenate([r[4 * b + qi]["ys"] for qi in range(4)], axis=0) for b in range(2)], axis=0)
    return (y_prompt, y_sample,
            cat("n_ckv")[:, None], cat("n_krope")[:, None], cat("n_nak")[:, None], cat("n_nav")[:, None],
            cat("n_gk")[:, None], cat("n_gv")[:, None], cat("n_sk")[:, None], cat("n_sv")[:, None])
```

```python
import contextlib
import os
import numpy as np
import ml_dtypes
import concourse.bass as bass
import concourse.mybir as mybir
from concourse.bass_utils import run_bass_kernel_spmd
from concourse.ap import AP

F32 = mybir.dt.float32
BF16 = mybir.dt.bfloat16
ALU = mybir.AluOpType
ACTF = mybir.ActivationFunctionType
AX = mybir.AxisListType

NCORES = 8
D = 1024
EPS = 1e-6
EV_IN = 2976
OD_IN = 2560
NPS = 4
SEQ = 256
SS = 1024
PAST = 256


class DSem:
    __slots__ = ("idx", "count")

    def __init__(self, idx):
        self.idx = idx
        self.count = 0


class Buf:
    __slots__ = ("name", "writer", "readers", "dq")

    def __init__(self, name):
        self.name = name
        self.writer = None
        self.readers = []
        self.dq = {}


class Sched:
    ENG = ("pe", "act", "dve", "pool", "sp")

    def __init__(self, nc):
        self.nc = nc
        self.ops = {e: [] for e in self.ENG}
        self.waited = {e: {} for e in self.ENG}
        self.dsems = []
        self.final_dma = []
        self.baton = None
        self._tls = None

    def stream_id(self):
        return getattr(self._tls, "me", None) if self._tls is not None else None

    def interleave(self, fa, fb, na=1, nb=1):
        import threading
        sem = [threading.Semaphore(0), threading.Semaphore(0)]
        done = [False, False]
        quota = [na, nb]
        cnt = [0]
        cur = [0]
        err = []

        def switch(me):
            other = 1 - me
            if done[other]:
                return
            cnt[0] += 1
            if cnt[0] >= quota[me]:
                cnt[0] = 0
                cur[0] = other
                sem[other].release()
                sem[me].acquire()

        def runner(me, f):
            sem[me].acquire()
            try:
                f()
            except BaseException as ex:
                err.append(ex)
            done[me] = True
            cur[0] = 1 - me
            sem[1 - me].release()

        tls = threading.local()
        self._tls = tls

        def baton():
            me = getattr(tls, "me", None)
            if me is not None:
                switch(me)

        def wrap(me, f):
            def g():
                tls.me = me
                f()
            return g

        ta = threading.Thread(target=runner, args=(0, wrap(0, fa)))
        tb_ = threading.Thread(target=runner, args=(1, wrap(1, fb)))
        old = self.baton
        self.baton = baton
        ta.start()
        tb_.start()
        sem[0].release()
        ta.join()
        tb_.join()
        self.baton = old
        self._tls = None
        if err:
            raise err[0]

    def _need(self, eng, tok, waits):
        if tok is None:
            return
        if tok[0] == "e":
            _, src, idx = tok
            if src == eng and eng == "pe":
                return
            key = ("e", src)
            val = idx
        else:
            _, b, val = tok
            key = ("d", id(b))
        w = self.waited[eng]
        if w.get(key, -1) >= val:
            return
        w[key] = val
        waits.append(tok)
        if tok[0] == "e":
            self.ops[tok[1]][tok[2]]["signal"] = True

    def _deps(self, eng, reads, writes):
        toks = []
        for b in reads:
            toks.append(b.writer)
        for b in writes:
            toks.append(b.writer)
            toks.extend(b.readers)
        best = {}
        for t in toks:
            if t is None:
                continue
            key = ("e", t[1]) if t[0] == "e" else ("d", id(t[1]))
            if key not in best or t[2] > best[key][2]:
                best[key] = t
        waits = []
        for t in best.values():
            self._need(eng, t, waits)
        return waits

    def op(self, eng, fn, reads=(), writes=()):
        if self.baton is not None:
            self.baton()
        waits = self._deps(eng, reads, writes)
        idx = len(self.ops[eng])
        self.ops[eng].append(dict(fn=fn, waits=waits, signal=False, dma=None))
        tok = ("e", eng, idx)
        for b in reads:
            b.readers.append(tok)
        for b in writes:
            b.writer = tok
            b.readers = []
        return tok

    def dma(self, eng, out, in_, owner, reads=(), writes=(), final=False, **kw):
        if self.baton is not None:
            self.baton()
        waits = self._deps(eng, reads, writes)
        ds = owner.dq.get(eng)
        if ds is None:
            ds = DSem(len(self.dsems))
            owner.dq[eng] = ds
            self.dsems.append(ds)
        ds.count += 16
        tok = ("d", ds, ds.count)

        def fn(e, out=out, in_=in_, kw=kw):
            return e.dma_start(out=out, in_=in_, **kw)

        self.ops[eng].append(dict(fn=fn, waits=waits, signal=False, dma=ds))
        for b in reads:
            b.readers.append(tok)
        for b in writes:
            b.writer = tok
            b.readers = []
        if final:
            self.final_dma.append(tok)
        return tok

    def dma_custom(self, eng, fn, owner, reads=(), writes=(), final=False):
        waits = self._deps(eng, reads, writes)
        ds = owner.dq.get(eng)
        if ds is None:
            ds = DSem(len(self.dsems))
            owner.dq[eng] = ds
            self.dsems.append(ds)
        ds.count += 16
        tok = ("d", ds, ds.count)
        self.ops[eng].append(dict(fn=fn, waits=waits, signal=False, dma=ds))
        for b in reads:
            b.readers.append(tok)
        for b in writes:
            b.writer = tok
            b.readers = []
        if final:
            self.final_dma.append(tok)
        return tok

    def emit(self):
        nc = self.nc
        with contextlib.ExitStack() as st:
            esem = {e: st.enter_context(nc.semaphore("es_" + e)) for e in self.ENG}
            dsem = [st.enter_context(nc.semaphore("ds_%d" % i)) for i in range(len(self.dsems))]
            block = st.enter_context(nc.Block())
            ordinal = {}
            for e in self.ENG:
                c = 0
                for i, o in enumerate(self.ops[e]):
                    if o["signal"]:
                        c += 1
                        ordinal[(e, i)] = c
            finals = list(self.final_dma)

            def run(e, h):
                for i, o in enumerate(self.ops[e]):
                    for t in o["waits"]:
                        if t[0] == "e":
                            h.wait_ge(esem[t[1]], ordinal[(t[1], t[2])])
                        else:
                            h.wait_ge(dsem[t[1].idx], t[2])
                    ins = o["fn"](h)
                    if o["dma"] is not None:
                        ins.then_inc(dsem[o["dma"].idx], 16)
                    elif o["signal"]:
                        ins.then_inc(esem[e], 1)
                if e == "sp":
                    done = {}
                    for t in finals:
                        done[t[1].idx] = max(done.get(t[1].idx, 0), t[2])
                    for k, v in done.items():
                        h.wait_ge(dsem[k], v)

            @block.tensor
            def _(h):
                run("pe", h)

            @block.scalar
            def _(h):
                run("act", h)

            @block.vector
            def _(h):
                run("dve", h)

            @block.gpsimd
            def _(h):
                run("pool", h)

            @block.sync
            def _(h):
                run("sp", h)


class Tl:
    def __init__(self, t, name):
        self.t = t
        self.b = Buf(name)

    def __getitem__(self, k):
        return self.t[k]


def bc(ap, shape):
    return ap.broadcast_to(list(shape))


def seg2(ap2d, stride, nseg):
    a = ap2d.ap
    return AP(ap2d.tensor, ap2d.offset, [list(a[0]), [stride, nseg], list(a[1])])


DEBUG = False


def build_program():
    nc = bass.Bass("TRN2", target_bir_lowering=False)
    S = Sched(nc)

    def din(name, shape, dt=F32):
        return nc.dram_tensor(name, list(shape), dt, kind="ExternalInput").ap()

    def dout(name, shape, dt=F32):
        return nc.dram_tensor(name, list(shape), dt, kind="ExternalOutput").ap()

    def dscr(name, shape, dt=F32):
        return nc.dram_tensor(name, list(shape), dt, kind="Internal").ap()

    xp = din("xp", [NPS * SEQ, D])
    xs = din("xs", [SS, D])
    c_s = din("c_s", [D])
    c_ctx = din("c_ctx", [D])
    ca_ckv = din("ca_ckv", [PAST, 128])
    ca_krope = din("ca_krope", [PAST, 32])
    ca_nak = din("ca_nak", [8, PAST, 64])
    ca_nav = din("ca_nav", [8, PAST, 64])
    ca_gk = din("ca_gk", [2, PAST, 64])
    ca_gv = din("ca_gv", [2, PAST, 64])
    ca_sk = din("ca_sk", [2, PAST, 64])
    ca_sv = din("ca_sv", [2, PAST, 64])
    norm_g = din("norm_g", [2, D])
    w_mod = din("w_mod", [2, D, 3 * D])
    b_mod = din("b_mod", [2, 3 * D])
    w_in_e = din("w_in_e", [D, EV_IN])
    w_q_up = din("w_q_up", [256, 768])
    w_kv_up = din("w_kv_up", [128, 1024])
    w_out_e = din("w_out_e", [D, D])
    w_in_o = din("w_in_o", [D, OD_IN])
    w_out_o = din("w_out_o", [D, D])
    gains = din("gains", [1, 968])
    rpbT = din("rpbT", [64, 8, 960])
    colvalid = din("colvalid", [128, 64])
    swm_own = din("swm_own", [8, 128, 256], BF16)
    own_idx = din("own_idx", [256, 1], mybir.dt.int32)
    rope_own = din("rope_own", [256, 2, 32])
    rope_e = din("rope_e", [SS, 2, 16])
    rope_o = din("rope_o", [SS, 2, 32])

    yp = dout("yp", [NPS * SEQ, D])
    ys = dout("ys", [256, D])
    n_ckv = dout("n_ckv", [NPS, SEQ, 128])
    n_krope = dout("n_krope", [NPS, SEQ, 32])
    n_nak = dout("n_nak", [NPS, 8, SEQ, 64])
    n_nav = dout("n_nav", [NPS, 8, SEQ, 64])
    n_gk = dout("n_gk", [NPS, 2, SEQ, 64])
    n_gv = dout("n_gv", [NPS, 2, SEQ, 64])
    n_sk = dout("n_sk", [NPS, 2, SEQ, 64])
    n_sv = dout("n_sv", [NPS, 2, SEQ, 64])

    x1p = dscr("x1p", [NPS * SEQ, D])
    x1s = dscr("x1s", [SS, D])
    ttd = dscr("ttd", [128, 8, 960], BF16)
    x1p_b = [Buf("x1p%d" % i) for i in range(NPS * SEQ // 128)]
    x1s_b = [Buf("x1s%d" % i) for i in range(SS // 128)]
    ttd_b = [Buf("ttd%d" % i) for i in range(8)]

    with contextlib.ExitStack() as st:
        def sb(name, shape, dt=F32):
            return Tl(st.enter_context(nc.sbuf_tensor(name, list(shape), dt)), name)

        def ps(name, shape, dt=F32):
            return Tl(st.enter_context(nc.psum_tensor(name, list(shape), dt)), name)

        w_in = sb("w_in", [128, 8, EV_IN], BF16)
        w_in_b = [Buf("w_in%d" % k) for k in range(8)]
        w_out = sb("w_out", [128, 8, D], BF16)
        w_out_b = [Buf("w_out%d" % k) for k in range(8)]
        wq = sb("wq", [128, 2, 768], BF16)
        wkv = sb("wkv", [128, 1024], BF16)

        KnT = sb("KnT", [128, 4, 1280], BF16)
        KrT = sb("KrT", [32, 1280], BF16)
        KbT = sb("KbT", [128, 4, 1280], BF16)
        Vg = sb("Vg", [128, 10, 16, 65], BF16)
        rsk = sb("rsk", [128, 10, 8], F32)
        QnT = sb("QnT", [128, 4, 256], BF16)
        QrT = sb("QrT", [32, 8, 256], BF16)
        QbT = sb("QbT", [128, 4, 256], BF16)
        Gb = sb("Gb", [128, 2, 1024], BF16)
        Ob = sb("Ob", [128, 2, 1024], BF16)
        ogT = sb("ogT", [128, 8, 256], BF16)
        hTb = sb("hTb", [128, 8, 256], BF16)
        xb = [sb("xb%d" % i, [128, D], F32) for i in range(2)]
        xtmp = sb("xtmp", [128, 512], F32)
        zb = [sb("z%d" % i, [128, 2048], F32) for i in range(2)]
        zg = [[Buf("z%d_%d" % (i, g)) for g in range(4)] for i in range(2)]
        U1 = sb("U1", [128, 1024], F32)
        U2 = sb("U2", [128, 1024], F32)
        xsb = sb("xsb", [128, 1024], BF16)
        tb = sb("tb", [128, 1024], BF16)
        tb2 = sb("tb2", [128, 256], BF16)
        ckvT = sb("ckvT", [128, 128], BF16)
        qlT = sb("qlT", [128, 2, 128], BF16)
        ident = sb("ident", [128, 128], BF16)
        Gn = sb("Gn", [128, 968], F32)
        cs_e = sb("cs_e", [128, 8, 2, 16], F32)
        cs_o = sb("cs_o", [128, 8, 2, 32], F32)
        cval = sb("cval", [128, 64], F32)
        swmb = [sb("swmb%d" % i, [128, 256], BF16) for i in range(1)]
        oidx = sb("oidx", [128, 2], mybir.dt.int32)
        cs_own = sb("cs_own", [128, 2, 2, 32], F32)
        gate_rep = [sb("gate_rep%d" % i, [128, D], F32) for i in range(2)]
        modc = sb("modc", [128, 16, 2], F32)
        bcol = sb("bcol", [128, 24], F32)
        gcol = sb("gcol", [128, 8], F32)
        ccol = sb("ccol", [128, 8, 2], F32)
        scT = sb("scT", [128, 8, 2], BF16)
        dadd = sb("dadd", [128, 16], F32)
        mhalf = sb("mhalf", [128, 16], F32)
        epsc = sb("epsc", [128, 2], F32)
        fence_t = sb("fence_t", [128, 2], F32)
        xr = sb("xr", [128, D], F32)
        tth = [sb("tth%d" % i, [128, 15, 64], BF16) for i in range(2)]
        PT = [sb("PT%d" % i, [128, 2, 256], BF16) for i in range(2)]
        Qbd = [sb("Qbd%d" % i, [128, 2, 256], BF16) for i in range(2)]
        qa_t = sb("qa_t", [128, 768], F32)
        qa_buf, qa_b = qa_t, qa_t.b
        smalls = {}

        def small(name, n):
            if name not in smalls:
                smalls[name] = sb("sm_" + name, [128, n], F32)
            return smalls[name]

        pz = [ps("pz%d" % i, [128, 512], F32) for i in range(2)]
        pt = [ps("pt%d" % i, [128, 1024], BF16) for i in range(2)]
        psc_t = [ps("psc%d" % i, [128, 2, 256], F32) for i in range(2)]
        psc_b = [[Buf("psc%d_%d" % (i, j)) for j in range(2)] for i in range(2)]
        po = [ps("po%d" % i, [128, 512], F32) for i in range(2)]

        rot = {}

        class PV:
            def __init__(self, i):
                self.ap = psc_t[i][:, :, :].rearrange("p a b -> p (a b)")
                self.b = psc_b[i][0]

            def __getitem__(self, k):
                return self.ap[k]

        pzs = None

        def pzbuf():
            sid = S.stream_id()
            if sid == 1:
                return pzs[nxt("pzs", 2)]
            if sid == 0:
                return pz[nxt("pz0", 2)]
            return pz[nxt("pz", 2)]

        pzs = [PV(0), PV(1)]

        def nxt(name, n):
            if name == "pt" and S.stream_id() is not None:
                return S.stream_id()
            v = rot.get(name, 0)
            rot[name] = (v + 1) % n
            return v

        def dve(fn, reads, writes):
            S.op("dve", fn, reads=reads, writes=writes)

        def pool(fn, reads, writes):
            S.op("pool", fn, reads=reads, writes=writes)

        def act(fn, reads, writes):
            S.op("act", fn, reads=reads, writes=writes)

        def pe(fn, reads, writes):
            S.op("pe", fn, reads=reads, writes=writes)

        def load(out, in_, owner, reads=(), extra_w=(), **kw):
            S.dma("sp", out, in_, owner, reads=list(reads), writes=[owner] + list(extra_w), **kw)

        def zalias(t):
            i = 0 if t is zb[0] else 1
            return zg[i]

        stg_b = [Buf("stg%d" % k) for k in range(4)]

        def stg_slot():
            k = nxt("stg", 4)
            return zb[k // 2], (k % 2) * 1024, stg_b[k]

        def z_fence():
            dve(lambda e: e.memset(fence_t[:], 0.0), [], [fence_t.b, zb[0].b, zb[1].b] + zg[0] + zg[1] + stg_b)

        def store(out, in_, owner, dst=(), final=False, **kw):
            S.dma("pool", out, in_, owner, reads=[owner], writes=list(dst), final=final, **kw)

        def transposes(srcs, src_bufs, dst_ap_fn, dst_buf, rows=128):
            p = pt[nxt("pt", 2)]
            n = len(srcs)
            for j, s_ap in enumerate(srcs):
                pe(lambda e, j=j, s_ap=s_ap, p=p: e.transpose(out=p[0:rows, j * 128:(j + 1) * 128], in_=s_ap, identity=ident[:]),
                   reads=list(src_bufs) + [ident.b], writes=[p.b])
            src_ps = p[0:rows, 0:n * 128].rearrange("p (j t) -> p j t", t=128)
            if nxt("tr_ev", 2) == 0:
                act(lambda e: e.activation(out=dst_ap_fn(), in_=src_ps, func=ACTF.Copy), reads=[p.b], writes=[dst_buf])
            else:
                dve(lambda e: e.tensor_copy(out=dst_ap_fn(), in_=src_ps), reads=[p.b], writes=[dst_buf])

        def rstd_of(src_ap, H, d, src_bufs, name, extra=None):
            ss = small("ss_" + name, H)
            rs = small("rs_" + name, H)
            sq = U1[:, 0:H * d].rearrange("p (h d) -> p h d", d=d)
            dve(lambda e: e.tensor_tensor(out=sq, in0=src_ap, in1=src_ap, op=ALU.mult), src_bufs, [U1.b])
            dve(lambda e: e.tensor_reduce(out=ss[:, 0:H], in_=sq, axis=AX.X, op=ALU.add), [U1.b], [ss.b])
            n = d
            if extra is not None:
                ex_ap, ex_buf, n = extra
                dve(lambda e: e.tensor_tensor(out=ss[:, 0:H], in0=ss[:, 0:H], in1=bc(ex_ap, [128, H]), op=ALU.add),
                    [ss.b, ex_buf], [ss.b])
            dve(lambda e: e.tensor_scalar(out=ss[:, 0:H], in0=ss[:, 0:H], scalar1=1.0 / n, scalar2=EPS, op0=ALU.mult, op1=ALU.add),
                reads=[ss.b], writes=[ss.b])
            pool(lambda e: e.tensor_tensor(out=rs[:, 0:H], in0=ss[:, 0:H], in1=mhalf[:, 0:H], op=ALU.pow),
                 reads=[ss.b, mhalf.b], writes=[rs.b])
            return rs

        def norm_heads(src_ap, H, d, src_bufs, gain_ap, out_ap, out_bufs, name, rs=None):
            if rs is None:
                rs = rstd_of(src_ap, H, d, src_bufs, name)
            tmp = U1[:, 0:H * d].rearrange("p (h d) -> p h d", d=d)
            dve(lambda e: e.tensor_tensor(out=tmp, in0=src_ap, in1=bc(rs[:, 0:H].unsqueeze(2), [128, H, d]), op=ALU.mult),
                reads=list(src_bufs) + [rs.b], writes=[U1.b])
            dve(lambda e: e.tensor_tensor(out=out_ap, in0=tmp, in1=bc(gain_ap.unsqueeze(1), [128, H, d]), op=ALU.mult),
                reads=[U1.b, Gn.b], writes=out_bufs)

        def rope(x_ap, H, half, cs_tile, ti, bufs):
            cos = bc(cs_tile[:, ti, 0, :].unsqueeze(1), [128, H, half])
            sin = bc(cs_tile[:, ti, 1, :].unsqueeze(1), [128, H, half])
            x1 = x_ap[:, :, 0:half]
            x2 = x_ap[:, :, half:2 * half]
            n = H * half
            ta = U2[:, 0:n].rearrange("p (h d) -> p h d", d=half)
            tb_ = U2[:, n:2 * n].rearrange("p (h d) -> p h d", d=half)
            tc = U2[:, 2 * n:3 * n].rearrange("p (h d) -> p h d", d=half)
            td = U2[:, 3 * n:4 * n].rearrange("p (h d) -> p h d", d=half)
            rb = list(bufs) + [cs_tile.b]
            dve(lambda e: e.tensor_tensor(out=ta, in0=x1, in1=cos, op=ALU.mult), reads=rb, writes=[U2.b])
            dve(lambda e: e.tensor_tensor(out=tb_, in0=x2, in1=sin, op=ALU.mult), reads=rb, writes=[U2.b])
            dve(lambda e: e.tensor_tensor(out=tc, in0=x1, in1=sin, op=ALU.mult), reads=rb, writes=[U2.b])
            dve(lambda e: e.tensor_tensor(out=td, in0=x2, in1=cos, op=ALU.mult), reads=rb, writes=[U2.b])
            dve(lambda e: e.tensor_tensor(out=x1, in0=ta, in1=tb_, op=ALU.subtract), reads=[U2.b], writes=list(bufs))
            dve(lambda e: e.tensor_tensor(out=x2, in0=tc, in1=td, op=ALU.add), reads=[U2.b], writes=list(bufs))

        identf = xtmp
        for qq in Qbd:
            pool(lambda e, qq=qq: e.memset(qq[:], 0.0), [], [qq.b])
        pool(lambda e: e.memset(identf[:, 0:128], 0.0), [], [identf.b])
        pool(lambda e: e.affine_select(out=identf[:, 0:128], in_=identf[:, 0:128], pattern=[[-1, 128]], compare_op=ALU.not_equal,
                                       fill=1.0, base=0, channel_multiplier=1), [identf.b], [identf.b])
        dve(lambda e: e.tensor_copy(out=ident[:], in_=identf[:, 0:128]), [identf.b], [ident.b])
        pool(lambda e: e.memset(mhalf[:], -0.5), [], [mhalf.b])
        pool(lambda e: e.memset(epsc[:, 0:1], EPS), [], [epsc.b])
        pool(lambda e: e.memset(epsc[:, 1:2], 1.0), [], [epsc.b])
        pool(lambda e: e.memset(Vg[:, :, :, 64:65], 1.0), [], [Vg.b])
        load(Gn[:], gains[0:1, :].to_broadcast([128, 968]), Gn.b)
        load(cs_e[:], rope_e.rearrange("(j p) a h -> p j a h", p=128), cs_e.b)
        load(cs_o[:], rope_o.rearrange("(j p) a h -> p j a h", p=128), cs_o.b)
        load(cval[:], colvalid[:, :], cval.b)
        load(oidx[:], own_idx.rearrange("(j p) o -> p (j o)", p=128), oidx.b, allow_slow_non_contiguous=True)
        load(cs_own[:], rope_own.rearrange("(j p) a h -> p j a h", p=128), cs_own.b)
        dve(lambda e: e.tensor_scalar(out=Gn[:, 384:480], in0=Gn[:, 384:480], scalar1=96.0 ** -0.5, scalar2=None, op0=ALU.mult), [Gn.b], [Gn.b])
        for lo in (576, 704, 832):
            dve(lambda e, lo=lo: e.tensor_scalar(out=Gn[:, lo:lo + 64], in0=Gn[:, lo:lo + 64], scalar1=0.125, scalar2=None, op0=ALU.mult), [Gn.b], [Gn.b])
        act(lambda e: e.activation(out=Gn[:, 960:968], in_=Gn[:, 960:968], func=ACTF.Exp), [Gn.b], [Gn.b])
        G_qa, G_kva = Gn[:, 0:256], Gn[:, 256:384]
        G_q, G_k = Gn[:, 384:480], Gn[:, 480:576]
        G_naq, G_nak = Gn[:, 576:640], Gn[:, 640:704]
        G_gq, G_gk = Gn[:, 704:768], Gn[:, 768:832]
        G_sq, G_sk = Gn[:, 832:896], Gn[:, 896:960]

        for h in range(8):
            tf = zb[h % 2]
            for half in range(2):
                load(tf[half * 64:(half + 1) * 64, 0:960], rpbT[:, h, :], tf.b, extra_w=zalias(tf))
            act(lambda e, tf=tf: e.activation(out=tf[:, 0:960], in_=tf[:, 0:960], func=ACTF.Exp), [tf.b], [tf.b])
            tt = tth[h % 2]
            dve(lambda e, tf=tf, tt=tt: e.tensor_tensor(out=tt[:], in0=tf[:, 0:960].rearrange("p (r c) -> p r c", c=64),
                                                    in1=bc(cval[:].unsqueeze(1), [128, 15, 64]), op=ALU.mult),
                [tf.b, cval.b], [tt.b])
            store(ttd[:, h, :], tt[:].rearrange("p r c -> p (r c)"), tt.b, dst=[ttd_b[h]])
        z_fence()

        def setup_mod(l):
            z_fence()
            load(bcol[:], b_mod[l].rearrange("(j p) -> p j", p=128), bcol.b, allow_slow_non_contiguous=True)
            load(gcol[:], norm_g[l].rearrange("(j p) -> p j", p=128), gcol.b, allow_slow_non_contiguous=True)
            brep = U2
            scRv = Ob[:, :, :].rearrange("p q n -> p (q n)").rearrange("p (k a t) -> p k a t", k=8, a=2)
            if True:
                load(ccol[:, :, 0], c_ctx.rearrange("(j p) -> p j", p=128), ccol.b, allow_slow_non_contiguous=True)
                load(ccol[:, :, 1], c_s.rearrange("(j p) -> p j", p=128), ccol.b, allow_slow_non_contiguous=True)
                t = small("silu", 16)
                tv = t[:, 0:16].rearrange("p (j a) -> p j a", a=2)
                act(lambda e: e.activation(out=tv, in_=ccol[:], func=ACTF.Exp, scale=-1.0), [ccol.b], [t.b])
                act(lambda e: e.activation(out=tv, in_=tv, func=ACTF.Ln, scale=1.0, bias=epsc[:, 1:2]), [t.b, epsc.b], [t.b])
                act(lambda e: e.activation(out=tv, in_=tv, func=ACTF.Exp, scale=-1.0), [t.b], [t.b])
                dve(lambda e: e.tensor_tensor(out=scT[:], in0=tv, in1=ccol[:], op=ALU.mult), [t.b, ccol.b], [scT.b])
                dve(lambda e: e.tensor_copy(out=scRv, in_=bc(scT[:].unsqueeze(3), [128, 8, 2, 128])), [scT.b], [Ob.b])
            load(brep[:], b_mod[l:l + 1, 2048:3072].to_broadcast([128, 1024]), brep.b)
            pm = psc_t[0]
            gacc = [[pz[0], pz[1]], [po[0], po[1]]]
            for kc in range(8):
                for piece in range(3):
                    stg, so, sbuf_ = stg_slot()
                    load(stg[:, so:so + 1024], w_mod[l, kc * 128:(kc + 1) * 128, piece * 1024:(piece + 1) * 1024], sbuf_)
                    wm = tb if nxt("wmrr", 2) == 0 else xsb
                    if nxt("castrr2", 2) == 0:
                        act(lambda e, stg=stg, so=so, wm=wm: e.activation(out=wm[:], in_=stg[:, so:so + 1024], func=ACTF.Copy), [sbuf_], [wm.b])
                    else:
                        dve(lambda e, stg=stg, so=so, wm=wm: e.tensor_copy(out=wm[:], in_=stg[:, so:so + 1024]), [sbuf_], [wm.b])
                    if piece < 2:
                        for j in range(8):
                            nch = piece * 8 + j
                            pe(lambda e, j=j, nch=nch, kc=kc, wm=wm: e.matmul(out=pm[:, 0, nch * 2:nch * 2 + 2], lhsT=wm[:, j * 128:(j + 1) * 128],
                                                                       rhs=scT[:, kc, :], start=(kc == 0 and nch == 0), stop=(kc == 7 and nch == 15),
                                                                       skip_group_check=True),
                               [wm.b, scT.b], [psc_b[0][0]])
                    else:
                        for cond in range(2):
                            for j in range(2):
                                g = gacc[cond][j]
                                pe(lambda e, g=g, j=j, cond=cond, kc=kc, wm=wm: e.matmul(out=g[:], lhsT=scRv[:, kc, cond, :], rhs=wm[:, j * 512:(j + 1) * 512],
                                                                                  start=(kc == 0), stop=(kc == 7)),
                                   [wm.b, Ob.b], [g.b])
            pmv = pm[:, 0, 0:32].rearrange("p (n a) -> p n a", a=2)
            dve(lambda e: e.tensor_tensor(out=modc[:], in0=pmv, in1=bc(bcol[:, 0:16].unsqueeze(2), [128, 16, 2]), op=ALU.add),
                [psc_b[0][0], bcol.b], [modc.b])
            dve(lambda e: e.scalar_tensor_tensor(out=modc[:, 8:16, :], in0=modc[:, 8:16, :], scalar=1.0, in1=bc(gcol[:].unsqueeze(2), [128, 8, 2]),
                                                 op0=ALU.add, op1=ALU.mult), [modc.b, gcol.b], [modc.b])
            for cond in range(2):
                for j in range(2):
                    g = gacc[cond][j]
                    dve(lambda e, g=g, cond=cond, j=j: e.tensor_tensor(out=gate_rep[cond][:, j * 512:(j + 1) * 512], in0=g[:],
                                                                       in1=brep[:, j * 512:(j + 1) * 512], op=ALU.add),
                        [g.b, brep.b], [gate_rep[cond].b])

        def load_cast(dst_ap_fn, src_ap, ncols, dst_buf):
            stg, so, sbuf_ = stg_slot()
            load(stg[:, so:so + ncols], src_ap, sbuf_)
            r = nxt("castrr", 5)
            if r in (0, 2):
                act(lambda e: e.activation(out=dst_ap_fn(), in_=stg[:, so:so + ncols], func=ACTF.Copy), [sbuf_], [dst_buf])
            elif r in (1, 3):
                dve(lambda e: e.tensor_copy(out=dst_ap_fn(), in_=stg[:, so:so + ncols]), [sbuf_], [dst_buf])
            else:
                pool(lambda e: e.tensor_copy(out=dst_ap_fn(), in_=stg[:, so:so + ncols]), [sbuf_], [dst_buf])

        def setup_weights(l):
            if l == 0:
                win_d, ncol, wout_d = w_in_e, EV_IN, w_out_e
            else:
                win_d, ncol, wout_d = w_in_o, OD_IN, w_out_o
            pieces = [(0, 992), (992, 992), (1984, 992)] if l == 0 else [(0, 1024), (1024, 1024), (2048, 512)]
            for kc in range(8):
                for (c0, cn) in pieces:
                    load_cast(lambda kc=kc, c0=c0, cn=cn: w_in[:, kc, c0:c0 + cn],
                              win_d[kc * 128:(kc + 1) * 128, c0:c0 + cn], cn, w_in_b[kc])
            if l == 0:
                for j in range(2):
                    load_cast(lambda j=j: wq[:, j, :], w_q_up[j * 128:(j + 1) * 128, :], 768, wq.b)
                load_cast(lambda: wkv[:], w_kv_up[:, :], 1024, wkv.b)
            for kc in range(8):
                load_cast(lambda kc=kc: w_out[:, kc, :], wout_d[kc * 128:(kc + 1) * 128, :], 1024, w_out_b[kc])
            pool(lambda e: e.memset(dadd[:], 0.0), [], [dadd.b])
            if l == 1:
                pool(lambda e: e.tensor_copy(out=dadd[:, 8:16], in_=Gn[:, 960:968]), [Gn.b], [dadd.b])
            z_fence()

        def gather_x(dst_t, col, src_bufs):
            S.dma_custom("pool", lambda e: e.indirect_dma_start(out=dst_t[:], out_offset=None, in_=x1s[:, :],
                                                                in_offset=bass.IndirectOffsetOnAxis(ap=oidx[:, col:col + 1], axis=0)),
                         dst_t.b, reads=list(src_bufs) + [oidx.b], writes=[dst_t.b])

        def load_x(x_src_ap, src_bufs, gather_col=None):
            xt = xb[nxt("xb", 2)]
            if gather_col is None:
                load(xt[:], x_src_ap, xt.b, reads=src_bufs)
            else:
                gather_x(xt, gather_col, src_bufs)
            return xt

        def make_hT(xt, cond, slot):
            ss = small("ss_x", 1)
            rs = small("rs_x", 1)
            act(lambda e: e.activation(out=xsb[:], in_=xt[:], func=ACTF.Square, accum_out=ss[:, 0:1]), [xt.b], [xsb.b, ss.b])
            act(lambda e: e.activation(out=ss[:, 0:1], in_=ss[:, 0:1], func=ACTF.Ln, scale=1.0 / D, bias=epsc[:, 0:1]), [ss.b, epsc.b], [ss.b])
            act(lambda e: e.activation(out=rs[:, 0:1], in_=ss[:, 0:1], func=ACTF.Exp, scale=-0.5), [ss.b], [rs.b])
            dve(lambda e: e.tensor_scalar(out=xsb[:], in0=xt[:], scalar1=rs[:, 0:1], scalar2=None, op0=ALU.mult), [xt.b, rs.b], [xsb.b])
            p = pt[nxt("pt", 2)]
            for kc in range(8):
                pe(lambda e, kc=kc, p=p: e.transpose(out=p[:, kc * 128:(kc + 1) * 128], in_=xsb[:, kc * 128:(kc + 1) * 128], identity=ident[:]),
                   [xsb.b, ident.b], [p.b])
            for kc in range(8):
                act(lambda e, kc=kc, p=p: e.activation(out=hTb[:, kc, slot * 128:(slot + 1) * 128], in_=p[:, kc * 128:(kc + 1) * 128],
                                                       func=ACTF.Identity, scale=modc[:, 8 + kc, cond:cond + 1], bias=modc[:, kc, cond:cond + 1]),
                    [p.b, modc.b], [hTb.b])

        def inproj(slot, groups, z, zgb):
            for (rhs_fn, N, zoff, gi) in groups:
                p = pzbuf()
                for kc in range(8):
                    pe(lambda e, kc=kc, p=p, rhs_fn=rhs_fn, N=N: e.matmul(out=p[:, 0:N], lhsT=hTb[:, kc, slot * 128:(slot + 1) * 128],
                                                                        rhs=rhs_fn(kc), start=(kc == 0), stop=(kc == 7)),
                       [hTb.b, w_in_b[kc]], [p.b])
                if nxt("ip_ev", 2) == 0:
                    act(lambda e, p=p, N=N, zoff=zoff: e.activation(out=z[:, zoff:zoff + N], in_=p[:, 0:N], func=ACTF.Copy), [p.b], [zgb[gi]])
                else:
                    dve(lambda e, p=p, N=N, zoff=zoff: e.tensor_copy(out=z[:, zoff:zoff + N], in_=p[:, 0:N]), [p.b], [zgb[gi]])

        def wcol(lo, n):
            return lambda kc: w_in[:, kc, lo:lo + n]

        def even_kv_post(z, zgb, chunk, ti, is_sample, is_ctx, out_seq=None, out_tile=None):
            A, B, C = zgb[0], zgb[1], zgb[2]
            kb3 = z[:, 512:1024].rearrange("p (h d) -> p h d", d=64)
            rs_kb = None
            if not is_ctx:
                rs_kb = rstd_of(kb3, 8, 64, [B], "kb")
            ssr = small("ssr", 1)
            junk = U1[:, 0:32]
            dve(lambda e: e.tensor_tensor(out=junk, in0=z[:, 128:160], in1=z[:, 128:160], op=ALU.mult), [A], [U1.b])
            dve(lambda e: e.tensor_reduce(out=ssr[:, 0:1], in_=junk, axis=AX.X, op=ALU.add), [U1.b], [ssr.b])
            ckv = z[:, 0:128]
            if not is_ctx:
                norm_heads(z[:, 0:128].rearrange("p (h d) -> p h d", d=128), 1, 128, [A], G_kva,
                           z[:, 0:128].rearrange("p (h d) -> p h d", d=128), [A], "ckv")
                if out_seq is not None:
                    store(n_ckv[out_seq, out_tile * 128:(out_tile + 1) * 128, :], z[:, 0:128], A, final=True)
                    store(n_krope[out_seq, out_tile * 128:(out_tile + 1) * 128, :], z[:, 128:160], A, final=True)
            dve(lambda e: e.tensor_copy(out=tb2[:, 0:128], in_=ckv), [A], [tb2.b])
            transposes([tb2[:, 0:128]], [tb2.b], lambda: ckvT[:].unsqueeze(1), ckvT.b)
            for j in range(2):
                p = pzbuf()
                pe(lambda e, p=p, j=j: e.matmul(out=p[:], lhsT=ckvT[:], rhs=wkv[:, j * 512:(j + 1) * 512], start=True, stop=True),
                   [ckvT.b, wkv.b], [p.b])
                act(lambda e, p=p, j=j: e.activation(out=U2[:, j * 512:(j + 1) * 512], in_=p[:], func=ACTF.Copy), [p.b], [U2.b])
            kv = U2[:, :].rearrange("p (h d) -> p h d", d=128)
            if not is_ctx:
                norm_heads(kb3, 8, 64, [B], G_nak, kb3, [B], "kb", rs=rs_kb)
                if out_seq is not None:
                    store(n_nak[out_seq, :, out_tile * 128:(out_tile + 1) * 128, :].rearrange("h t d -> t h d"), kb3, B, final=True)
                    store(n_nav[out_seq, :, out_tile * 128:(out_tile + 1) * 128, :].rearrange("h t d -> t h d"),
                          z[:, 1024:1536].rearrange("p (h d) -> p h d", d=64), C, final=True)
            dve(lambda e: e.tensor_copy(out=tb[:, 512:1024], in_=z[:, 512:1024]), [B], [tb.b])
            transposes([tb[:, 512 + j * 128:512 + (j + 1) * 128] for j in range(4)], [tb.b],
                       lambda: KbT[:, :, chunk * 128:(chunk + 1) * 128], KbT.b)
            act(lambda e: e.activation(out=Vg[:, chunk, 8:16, 0:64], in_=z[:, 1024:1536].rearrange("p (h d) -> p h d", d=64), func=ACTF.Copy), [C], [Vg.b])
            dve(lambda e: e.tensor_copy(out=Vg[:, chunk, 0:8, 0:64], in_=kv[:, :, 64:128]), [U2.b], [Vg.b])
            rs = rstd_of(kv[:, :, 0:64], 8, 64, [U2.b], "ka", extra=(ssr[:, 0:1], ssr.b, 96))
            dve(lambda e: e.tensor_tensor(out=tb[:, 0:512].rearrange("p (h d) -> p h d", d=64), in0=kv[:, :, 0:64],
                                          in1=bc(G_k[:, 0:64].unsqueeze(1), [128, 8, 64]), op=ALU.mult), [U2.b, Gn.b], [tb.b])
            transposes([tb[:, j * 128:(j + 1) * 128] for j in range(4)], [tb.b],
                       lambda: KnT[:, :, chunk * 128:(chunk + 1) * 128], KnT.b)
            dve(lambda e: e.tensor_copy(out=rsk[:, chunk, :], in_=rs[:, 0:8]), [rs.b], [rsk.b])
            kr = small("kr", 32)
            dve(lambda e: e.tensor_tensor(out=kr[:, 0:32], in0=z[:, 128:160], in1=G_k[:, 64:96], op=ALU.mult), [A, Gn.b], [kr.b])
            if is_sample and not is_ctx:
                rope(kr[:, 0:32].rearrange("p (h d) -> p h d", h=1), 1, 16, cs_e, ti, [kr.b])
            dve(lambda e: e.tensor_copy(out=tb2[:, 128:160], in_=kr[:, 0:32]), [kr.b], [tb2.b])
            transposes([tb2[:, 128:160]], [tb2.b], lambda: KrT[:, chunk * 128:(chunk + 1) * 128].unsqueeze(1), KrT.b, rows=32)

        EV_KV_GROUPS = [(wcol(256, 160), 160, 0, 0), (wcol(1440, 512), 512, 512, 1), (wcol(1952, 512), 512, 1024, 2)]
        EV_Q_GROUPS = [(wcol(0, 256), 256, 0, 0), (wcol(416, 512), 512, 512, 1), (wcol(928, 512), 512, 1024, 2), (wcol(2464, 512), 512, 1536, 3)]

        def gates(z, zgb, gidx_off, qt):
            for n, (gi, zoff) in enumerate(gidx_off):
                t = U2[:, n * 512:(n + 1) * 512]
                act(lambda e, t=t, zoff=zoff: e.activation(out=t, in_=z[:, zoff:zoff + 512], func=ACTF.Exp, scale=-1.0), [zgb[gi]], [U2.b])
                act(lambda e, t=t: e.activation(out=t, in_=t, func=ACTF.Ln, scale=1.0, bias=epsc[:, 1:2]), [U2.b, epsc.b], [U2.b])
                act(lambda e, t=t: e.activation(out=t, in_=t, func=ACTF.Exp, scale=-1.0), [U2.b], [U2.b])
                dve(lambda e, t=t, zoff=zoff, n=n: e.tensor_tensor(out=Gb[:, qt, n * 512:(n + 1) * 512], in0=t, in1=z[:, zoff:zoff + 512], op=ALU.mult),
                    [U2.b, zgb[gi]], [Gb.b])

        def even_q_post(z, zgb, qt, ti, is_sample):
            rs_qb = rstd_of(z[:, 1024:1536].rearrange("p (h d) -> p h d", d=64), 8, 64, [zgb[2]], "qb")
            ql3 = z[:, 0:256].rearrange("p (h d) -> p h d", d=256)
            norm_heads(ql3, 1, 256, [zgb[0]], G_qa, tb2[:, 0:256].rearrange("p (h d) -> p h d", d=256), [tb2.b], "ql")
            transposes([tb2[:, 0:128], tb2[:, 128:256]], [tb2.b], lambda: qlT[:], qlT.b)
            for n in range(2):
                p = pzbuf()
                for j in range(2):
                    pe(lambda e, p=p, j=j, n=n: e.matmul(out=p[:, 0:384], lhsT=qlT[:, j, :], rhs=wq[:, j, n * 384:(n + 1) * 384],
                                                       start=(j == 0), stop=(j == 1)), [qlT.b, wq.b], [p.b])
                act(lambda e, p=p, n=n: e.activation(out=qa_buf[:, n * 384:(n + 1) * 384], in_=p[:, 0:384], func=ACTF.Copy), [p.b], [qa_b])
            q3 = qa_buf[:, 0:768].rearrange("p (h d) -> p h d", d=96)
            rs_qa = rstd_of(q3, 8, 96, [qa_b], "qa")
            qb3 = z[:, 1024:1536].rearrange("p (h d) -> p h d", d=64)
            norm_heads(qb3, 8, 64, [zgb[2]], G_naq, tb[:, 512:1024].rearrange("p (h d) -> p h d", d=64), [tb.b], "qb", rs=rs_qb)
            transposes([tb[:, 512 + j * 128:512 + (j + 1) * 128] for j in range(4)], [tb.b],
                       lambda: QbT[:, :, qt * 128:(qt + 1) * 128], QbT.b)
            norm_heads(q3, 8, 96, [qa_b], G_q, q3, [qa_b], "qa", rs=rs_qa)
            if is_sample:
                rope(q3[:, :, 64:96], 8, 16, cs_e, ti, [qa_b])
            dve(lambda e: e.tensor_copy(out=tb[:, 0:512].rearrange("p (h d) -> p h d", d=64), in_=q3[:, :, 0:64]), [qa_b], [tb.b])
            transposes([tb[:, j * 128:(j + 1) * 128] for j in range(4)], [tb.b],
                       lambda: QnT[:, :, qt * 128:(qt + 1) * 128], QnT.b)
            dve(lambda e: e.tensor_copy(out=tb2[:, 0:256].rearrange("p (h d) -> p h d", d=32), in_=q3[:, :, 64:96]), [qa_b], [tb2.b])
            transposes([tb2[:, h * 32:(h + 1) * 32] for h in range(8)], [tb2.b],
                       lambda: QrT[:, :, qt * 128:(qt + 1) * 128], QrT.b, rows=32)
            gates(z, zgb, [(1, 512), (3, 1536)], qt)


        def odd_kv_rhs(kc):
            return seg2(w_in[:, kc, 512:768], 1280, 2)

        OD_KV_GROUPS = [(odd_kv_rhs, 512, 0, 0)]
        OD_Q_GROUPS = [(wcol(0, 512), 512, 0, 0), (wcol(768, 512), 512, 512, 1), (wcol(1280, 512), 512, 1024, 2), (wcol(2048, 512), 512, 1536, 3)]

        def inproj_odd_kv(slot, z, zgb):
            p = pzbuf()
            for kc in range(8):
                pe(lambda e, kc=kc, p=p: e.matmul(out=p[:, 0:512].rearrange("p (a b) -> p a b", b=256), lhsT=hTb[:, kc, slot * 128:(slot + 1) * 128],
                                                 rhs=odd_kv_rhs(kc), start=(kc == 0), stop=(kc == 7)), [hTb.b, w_in_b[kc]], [p.b])
            act(lambda e, p=p: e.activation(out=z[:, 0:512], in_=p[:, 0:512], func=ACTF.Copy), [p.b], [zgb[0]])

        def odd_kv_post(z, zgb, chunk, ti, is_sample, is_ctx, out_seq=None, out_tile=None):
            A = zgb[0]
            for gi, (koff, voff, gain, kslot, vslot, ok, ov) in enumerate(((0, 128, G_gk, 0, 0, n_gk, n_gv), (256, 384, G_sk, 2, 2, n_sk, n_sv))):
                k3 = z[:, koff:koff + 128].rearrange("p (h d) -> p h d", d=64)
                v3 = z[:, voff:voff + 128].rearrange("p (h d) -> p h d", d=64)
                if not is_ctx:
                    norm_heads(k3, 2, 64, [A], gain, k3, [A], "ko%d" % gi)
                    if out_seq is not None:
                        store(ok[out_seq, :, out_tile * 128:(out_tile + 1) * 128, :].rearrange("h t d -> t h d"), k3, A, final=True)
                        store(ov[out_seq, :, out_tile * 128:(out_tile + 1) * 128, :].rearrange("h t d -> t h d"), v3, A, final=True)
                    if is_sample:
                        kr = small("kro%d" % gi, 128)
                        kr3 = kr[:, 0:128].rearrange("p (h d) -> p h d", d=64)
                        dve(lambda e, kr3=kr3, k3=k3: e.tensor_copy(out=kr3, in_=k3), [A], [kr.b])
                        rope(kr3, 2, 32, cs_o, ti, [kr.b])
                        ksrc, ksb = kr3, kr.b
                    else:
                        ksrc, ksb = k3, A
                else:
                    ksrc, ksb = k3, A
                o4 = tb2[:, 0:256].rearrange("p (h r d) -> p h r d", r=2, d=64)
                dve(lambda e, o4=o4, ksrc=ksrc: e.tensor_copy(out=o4, in_=bc(ksrc.unsqueeze(2), [128, 2, 2, 64])), [ksb], [tb2.b])
                transposes([tb2[:, 0:128], tb2[:, 128:256]], [tb2.b],
                           lambda kslot=kslot: KnT[:, kslot:kslot + 2, chunk * 128:(chunk + 1) * 128], KnT.b)
                pool(lambda e, v3=v3, vslot=vslot: e.tensor_copy(out=Vg[:, chunk, vslot:vslot + 2, 0:64], in_=v3), [A], [Vg.b])

        def odd_q_post(z, zgb, qt, ti, is_sample, cs=None):
            cs_t, cs_i = (cs_o, ti) if cs is None else cs
            rs_pre = [rstd_of(z[:, off:off + 512].rearrange("p (h d) -> p h d", d=64), 8, 64, [zgb[0] if gi == 0 else zgb[2]], "qo%d" % gi)
                      for gi, off in enumerate((0, 1024))]
            for gi, (off, gain, dstT) in enumerate(((0, G_gq, QnT), (1024, G_sq, QbT))):
                q3 = z[:, off:off + 512].rearrange("p (h d) -> p h d", d=64)
                zb_ = zgb[0] if gi == 0 else zgb[2]
                if is_sample:
                    norm_heads(q3, 8, 64, [zb_], gain, q3, [zb_], "qo%d" % gi, rs=rs_pre[gi])
                    rope(q3, 8, 32, cs_t, cs_i, [zb_])
                    dve(lambda e, q3=q3: e.tensor_copy(out=tb[:, 0:512].rearrange("p (h d) -> p h d", d=64), in_=q3), [zb_], [tb.b])
                else:
                    norm_heads(q3, 8, 64, [zb_], gain, tb[:, 0:512].rearrange("p (h d) -> p h d", d=64), [tb.b], "qo%d" % gi, rs=rs_pre[gi])
                transposes([tb[:, j * 128:(j + 1) * 128] for j in range(4)], [tb.b],
                           lambda dstT=dstT: dstT[:, :, qt * 128:(qt + 1) * 128], dstT.b)
            gates(z, zgb, [(1, 512), (3, 1536)], qt)

        def attention(l, is_sample, qb_idx, own=False):
            nlat = 8 if is_sample else 2
            ctx_chunks = [8, 9] if is_sample else []
            R = 4 * qb_idx

            def lo(r):
                return min(max(r - 4, 0), 8)

            items = []
            for pr in range(8):
                h0 = 2 * pr
                if l == 0:
                    if h0 < 8:
                        kind, chunks = "mla", list(range(nlat)) + ctx_chunks
                    else:
                        kind = "na"
                        if is_sample:
                            lat = {0: [0, 1, 2, 3], 1: [0, 1, 2, 3, 4, 5], 2: [2, 3, 4, 5, 6, 7], 3: [4, 5, 6, 7]}[qb_idx]
                        else:
                            lat = [0, 1]
                        chunks = lat + ctx_chunks
                else:
                    if h0 < 8:
                        kind, chunks = "gqa", list(range(nlat)) + ctx_chunks
                    else:
                        kind = "swa"
                        if own:
                            lat = list(range(8))
                        elif is_sample:
                            lat = [j for j in (2 * qb_idx - 1, 2 * qb_idx, 2 * qb_idx + 1, 2 * qb_idx + 2) if 0 <= j <= 7]
                        else:
                            lat = [0, 1]
                        chunks = lat + ctx_chunks
                for ci, j in enumerate(chunks):
                    items.append(dict(pr=pr, kind=kind, j=j, ci=ci, nch=len(chunks)))
            pair_tt = {}
            pair_q = {}

            def front(it):
                pr, kind, j = it["pr"], it["kind"], it["j"]
                h0 = 2 * pr
                if kind == "na" and is_sample and it["ci"] == 0:
                    tts = []
                    for hh in range(2):
                        tt = tth[hh]
                        load(tt[:].rearrange("p r c -> p (r c)"), ttd[:, h0 + hh - 8, :], tt.b, reads=[ttd_b[h0 + hh - 8]])
                        tts.append(tt)
                    pair_tt[pr] = tts
                bi = nxt("psc", 2)
                pst, psb = psc_t[bi], psc_b[bi][0]
                ks = slice(j * 128, (j + 1) * 128)
                if it["ci"] == 0:
                    qbd = Qbd[nxt("qbd", 2)]
                    qsrc = QnT if kind in ("mla", "gqa") else QbT
                    qpi = pr if kind in ("mla", "gqa") else pr - 4
                    dve(lambda e: e.tensor_copy(out=qbd[0:64, 0, :], in_=qsrc[0:64, qpi, :]), [qsrc.b], [qbd.b])
                    dve(lambda e: e.tensor_copy(out=qbd[64:128, 1, :], in_=qsrc[64:128, qpi, :]), [qsrc.b], [qbd.b])
                    pair_q[pr] = qbd
                qbd = pair_q[pr]
                if kind == "mla":
                    ksrc, kpi = KnT, pr
                elif kind == "na":
                    ksrc, kpi = KbT, pr - 4
                elif kind == "gqa":
                    ksrc, kpi = KnT, h0 // 4
                else:
                    ksrc, kpi = KnT, 2 + (h0 - 8) // 4
                pe(lambda e: e.matmul(out=pst[:], lhsT=ksrc[:, kpi, ks], rhs=qbd[:], start=True, stop=(kind != "mla")),
                   [ksrc.b, qbd.b], [psb])
                if kind == "mla":
                    pe(lambda e: e.matmul(out=pst[:], lhsT=KrT[:, ks], rhs=QrT[:, h0:h0 + 2, :], start=False, stop=True), [KrT.b, QrT.b], [psb])
                P = PT[nxt("PT", 2)]
                if kind == "mla":
                    for hh in range(2):
                        act(lambda e, hh=hh: e.activation(out=P[:, hh, :], in_=pst[:, hh, :], func=ACTF.Exp, scale=rsk[:, j, h0 + hh:h0 + hh + 1]),
                            [psb, rsk.b], [P.b])
                else:
                    act(lambda e: e.activation(out=P[:], in_=pst[:], func=ACTF.Exp), [psb], [P.b])
                if is_sample and j < 8:
                    if kind == "na":
                        for hh in range(2):
                            tt = pair_tt[pr][hh]
                            for a in range(2):
                                rp = 2 * j + a
                                val = [lo(R + i) <= rp < lo(R + i) + 8 for i in range(4)]
                                vi = [i for i in range(4) if val[i]]
                                psl = slice(a * 64, (a + 1) * 64)
                                if vi:
                                    i0, n = vi[0], len(vi)
                                    d0 = 7 - rp + R + i0
                                    eng = dve
                                    eng(lambda e, psl=psl, i0=i0, n=n, d0=d0, hh=hh, tt=tt: e.tensor_tensor(
                                        out=P[psl, hh, i0 * 64:(i0 + n) * 64], in0=P[psl, hh, i0 * 64:(i0 + n) * 64],
                                        in1=tt[psl, d0:d0 + n, :].rearrange("p r c -> p (r c)"), op=ALU.mult), [P.b, tt.b], [P.b])
                                for i in range(4):
                                    if not val[i]:
                                        pool(lambda e, psl=psl, i=i, hh=hh: e.memset(P[psl, hh, i * 64:(i + 1) * 64], 0.0), [], [P.b])
                    elif kind == "swa":
                        mt = swmb[0]
                        load(mt[:], swm_own[j, :, :], mt.b)
                        dve(lambda e: e.tensor_tensor(out=P[:], in0=P[:], in1=bc(mt[:].unsqueeze(1), [128, 2, 256]), op=ALU.mult),
                             [P.b, mt.b], [P.b])
                it["P"] = P

            def back(it):
                pr, kind, j, ci, nch, P = it["pr"], it["kind"], it["j"], it["ci"], it["nch"], it["P"]
                h0 = 2 * pr
                pacc = po[pr % 2]
                pv4 = pacc[:, 0:260].rearrange("p (q h d) -> p q h d", q=2, h=2)
                for hh in range(2):
                    h = h0 + hh
                    if kind in ("mla", "na"):
                        vsl = h
                    elif kind == "gqa":
                        vsl = h // 4
                    else:
                        vsl = 2 + (h - 8) // 4
                    for q in range(2):
                        pe(lambda e, q=q, hh=hh, vsl=vsl: e.matmul(out=pv4[:, q, hh, :], lhsT=P[:, hh, q * 128:(q + 1) * 128], rhs=Vg[:, j, vsl, :],
                                                                  start=(ci == 0 and hh == 0 and q == 0), stop=(ci == nch - 1 and hh == 1 and q == 1),
                                                                  skip_group_check=True), [P.b, Vg.b], [pacc.b])
                if ci == nch - 1:
                    den = small("den", 4)
                    dv = den[:, 0:4].rearrange("p (q h) -> p q h", q=2)
                    dve(lambda e: e.tensor_tensor(out=dv, in0=pv4[:, :, :, 64], in1=bc(dadd[:, h0:h0 + 2].unsqueeze(1), [128, 2, 2]), op=ALU.add),
                        [pacc.b, dadd.b], [den.b])
                    dve(lambda e: e.reciprocal(out=dv, in_=dv), [den.b], [den.b])
                    ov = Ob[:, :, h0 * 64:(h0 + 2) * 64].rearrange("p q (h d) -> p q h d", d=64)
                    gv = Gb[:, :, h0 * 64:(h0 + 2) * 64].rearrange("p q (h d) -> p q h d", d=64)
                    o4 = xtmp[:, 0:256].rearrange("p (q h d) -> p q h d", q=2, h=2)
                    dve(lambda e: e.tensor_tensor(out=o4, in0=pv4[:, :, :, 0:64], in1=bc(dv.unsqueeze(3), [128, 2, 2, 64]), op=ALU.mult),
                        [pacc.b, den.b], [xtmp.b])
                    dve(lambda e: e.tensor_tensor(out=ov, in0=o4, in1=gv, op=ALU.mult), [xtmp.b, Gb.b], [Ob.b])

            LA = 1
            for idx in range(len(items) + LA):
                if idx < len(items):
                    front(items[idx])
                if idx - LA >= 0:
                    back(items[idx - LA])
            for q in range(2):
                transposes([Ob[:, q, kc * 128:(kc + 1) * 128] for kc in range(8)], [Ob.b],
                           lambda q=q: ogT[:, :, q * 128:(q + 1) * 128], ogT.b)

        def outproj_residual(cond, slot, x_src_ap, src_bufs, dst_ap, dst_bufs, final, gather_col=None):
            if gather_col is None:
                S.dma("pool", xr[:], x_src_ap, xr.b, reads=list(src_bufs), writes=[xr.b])
            else:
                gather_x(xr, gather_col, src_bufs)
            for n in range(2):
                p = pzbuf()
                for kc in range(8):
                    pe(lambda e, p=p, kc=kc, n=n: e.matmul(out=p[:], lhsT=ogT[:, kc, slot * 128:(slot + 1) * 128], rhs=w_out[:, kc, n * 512:(n + 1) * 512],
                                                         start=(kc == 0), stop=(kc == 7)), [ogT.b, w_out_b[kc]], [p.b])
                dve(lambda e, p=p, n=n: e.tensor_tensor(out=xtmp[:], in0=p[:], in1=gate_rep[cond][:, n * 512:(n + 1) * 512], op=ALU.mult),
                    [p.b, gate_rep[cond].b], [xtmp.b])
                dve(lambda e, n=n: e.tensor_tensor(out=xr[:, n * 512:(n + 1) * 512], in0=xr[:, n * 512:(n + 1) * 512], in1=xtmp[:], op=ALU.add),
                    [xtmp.b, xr.b], [xr.b])
            store(dst_ap, xr[:], xr.b, dst=dst_bufs, final=final)

        dbg_done = set()

        def dbg(name, tile_ap, buf, shape, dt=F32):
            if not DEBUG or name in dbg_done:
                return
            dbg_done.add(name)
            d = dout("dbg_" + name, shape, dt)
            S.dma("sp", d, tile_ap, buf, reads=[buf], final=True)

        def run_jobs(jobs):
            prev = None
            prev_tail = None
            if jobs[0][0] is not None:
                jobs[0][0]()
            for i, (pre, s1, s2, tail) in enumerate(jobs):
                if i + 1 < len(jobs) and jobs[i + 1][0] is not None:
                    jobs[i + 1][0]()
                if prev is None:
                    s1()
                elif os.environ.get('K_NOINT'):
                    s1()
                    prev()
                else:
                    S.interleave(s1, prev, *((1, 1) if getattr(prev, '__name__', '') in ('p1_s2', 'cx_s2') else (3, 2)))
                if prev_tail is not None:
                    prev_tail()
                prev = s2
                prev_tail = tail
            if prev is not None:
                prev()
            if prev_tail is not None:
                prev_tail()

        for l in range(2):
            setup_mod(l)
            dbg("modc", modc[:].rearrange("p a b -> p (a b)"), modc.b, [128, 32])
            setup_weights(l)
            q_groups = EV_Q_GROUPS if l == 0 else OD_Q_GROUPS
            last = (l == 1)
            jobs = []
            seqs = [("p", s_) for s_ in range(NPS)] + [("s", 0)]
            for kind, s_ in seqs:
                is_sample = kind == "s"
                cond = 1 if is_sample else 0
                ntile = 8 if is_sample else 2
                if is_sample:
                    src = xs if l == 0 else x1s
                    srcb = (lambda t: []) if l == 0 else (lambda t: [x1s_b[t]])
                    dst = ys if last else x1s
                    dstb = (lambda t: []) if last else (lambda t: [x1s_b[t]])
                    base = 0
                else:
                    src = xp if l == 0 else x1p
                    srcb = (lambda t: []) if l == 0 else (lambda t: [x1p_b[t]])
                    dst = yp if last else x1p
                    dstb = (lambda t: []) if last else (lambda t: [x1p_b[t]])
                    base = s_ * 2
                for ti in range(ntile):
                    st_ = {}

                    def p1_pre(ti=ti, st_=st_, src=src, srcb=srcb, base=base):
                        gt = base + ti
                        st_["xt"] = load_x(src[gt * 128:(gt + 1) * 128, :], srcb(gt))

                    def p1_s1(ti=ti, st_=st_, src=src, srcb=srcb, base=base, cond=cond):
                        slot = nxt("hslot", 2)
                        make_hT(st_["xt"], cond, slot)
                        zi = nxt("z", 2)
                        st_["z"], st_["zgb"] = zb[zi], zg[zi]
                        if l == 0:
                            inproj(slot, EV_KV_GROUPS, zb[zi], zg[zi])
                        else:
                            inproj_odd_kv(slot, zb[zi], zg[zi])

                    def p1_s2(ti=ti, st_=st_, is_sample=is_sample, s_=s_):
                        if l == 0:
                            even_kv_post(st_["z"], st_["zgb"], ti, ti, is_sample, False, None if is_sample else s_, ti)
                        else:
                            odd_kv_post(st_["z"], st_["zgb"], ti, ti, is_sample, False, None if is_sample else s_, ti)

                    jobs.append((p1_pre, p1_s1, p1_s2, None))
                if is_sample:
                    for cc in range(2):
                        st_ = {}

                        def cx_s1(cc=cc, st_=st_):
                            zi = nxt("z", 2)
                            z, zgb = zb[zi], zg[zi]
                            st_["z"], st_["zgb"] = z, zgb
                            tsl = slice(cc * 128, (cc + 1) * 128)
                            if l == 0:
                                load(z[:, 0:128], ca_ckv[tsl, :], zgb[0])
                                load(z[:, 128:160], ca_krope[tsl, :], zgb[0])
                                load(z[:, 512:1024].rearrange("p (h d) -> p h d", d=64), ca_nak[:, tsl, :].rearrange("h t d -> t h d"), zgb[1])
                                load(z[:, 1024:1536].rearrange("p (h d) -> p h d", d=64), ca_nav[:, tsl, :].rearrange("h t d -> t h d"), zgb[2])
                            else:
                                load(z[:, 0:128].rearrange("p (h d) -> p h d", d=64), ca_gk[:, tsl, :].rearrange("h t d -> t h d"), zgb[0])
                                load(z[:, 128:256].rearrange("p (h d) -> p h d", d=64), ca_gv[:, tsl, :].rearrange("h t d -> t h d"), zgb[0])
                                load(z[:, 256:384].rearrange("p (h d) -> p h d", d=64), ca_sk[:, tsl, :].rearrange("h t d -> t h d"), zgb[0])
                                load(z[:, 384:512].rearrange("p (h d) -> p h d", d=64), ca_sv[:, tsl, :].rearrange("h t d -> t h d"), zgb[0])

                        def cx_s2(cc=cc, st_=st_):
                            if l == 0:
                                even_kv_post(st_["z"], st_["zgb"], 8 + cc, 0, True, True)
                            else:
                                odd_kv_post(st_["z"], st_["zgb"], 8 + cc, 0, True, True)

                        jobs.append((None, cx_s1, cx_s2, None))
                own = is_sample and l == 1
                allsrc = [b_ for t_ in range(8) for b_ in srcb(t_)] if own else None
                for qb_idx in range(1 if own else ntile // 2):
                    for qt in range(2):
                        st_ = {}

                        def p2_pre(qb_idx=qb_idx, qt=qt, st_=st_, src=src, srcb=srcb, base=base, own=own, allsrc=allsrc):
                            gt = base + qb_idx * 2 + qt
                            if own:
                                st_["xt"] = load_x(None, allsrc, gather_col=qt)
                            else:
                                st_["xt"] = load_x(src[gt * 128:(gt + 1) * 128, :], srcb(gt))

                        def p2_s1(qb_idx=qb_idx, qt=qt, st_=st_, src=src, srcb=srcb, base=base, cond=cond, own=own, allsrc=allsrc):
                            slot = nxt("hslot", 2)
                            st_["slot"] = slot
                            make_hT(st_["xt"], cond, slot)
                            zi = nxt("z", 2)
                            st_["z"], st_["zgb"] = zb[zi], zg[zi]
                            inproj(slot, q_groups, zb[zi], zg[zi])

                        def p2_s2(qb_idx=qb_idx, qt=qt, st_=st_, is_sample=is_sample, src=src, srcb=srcb, dst=dst, dstb=dstb, base=base, cond=cond,
                                  own=own, allsrc=allsrc):
                            ti = qb_idx * 2 + qt
                            if l == 0:
                                even_q_post(st_["z"], st_["zgb"], qt, ti, is_sample)
                            elif own:
                                odd_q_post(st_["z"], st_["zgb"], qt, ti, is_sample, cs=(cs_own, qt))
                            else:
                                odd_q_post(st_["z"], st_["zgb"], qt, ti, is_sample)
                            if qt == 1:
                                st_["tail"] = True

                        def p2_tail(qb_idx=qb_idx, qt=qt, st_=st_, is_sample=is_sample, src=src, srcb=srcb, dst=dst, dstb=dstb, base=base, cond=cond,
                                    own=own, allsrc=allsrc):
                            if qt == 1:
                                attention(l, is_sample, qb_idx, own=own)
                                for q2 in range(2):
                                    gt = base + qb_idx * 2 + q2
                                    if own:
                                        outproj_residual(cond, q2, None, allsrc, ys[q2 * 128:(q2 + 1) * 128, :], [], True, gather_col=q2)
                                    else:
                                        outproj_residual(cond, q2, src[gt * 128:(gt + 1) * 128, :], srcb(gt),
                                                         dst[gt * 128:(gt + 1) * 128, :], dstb(gt), last)

                        jobs.append((p2_pre, p2_s1, p2_s2, p2_tail if qt == 1 else None))
            run_jobs(jobs)
        S.emit()
    return nc


_NC_CACHE = {}


def _consts():
    c = np.arange(64)
    cs = np.clip(c - 8, 0, 48)
    cv = ((c[:, None] >= cs[None, :]) & (c[:, None] < cs[None, :] + 16)).astype(np.float32)
    colvalid = np.concatenate([cv, cv], axis=0)
    k = np.arange(128)[:, None]
    q = np.arange(256)[None, :]
    swamask = []
    for qi in range(4):
        swamask.append(np.stack([(np.abs((256 * qi + q) - (128 * j + k)) <= 128) for j in range(8)], axis=0)
                       .astype(np.float32).astype(ml_dtypes.bfloat16))

    def rope_tab(rot):
        quarter = rot // 4
        t = np.arange(SS)
        inv = (10000.0 ** (-np.arange(quarter, dtype=np.float32) / quarter)).astype(np.float32)
        row = (t // 64).astype(np.float32)[:, None] * inv
        col = (t % 64).astype(np.float32)[:, None] * inv
        ang = np.concatenate([row, col], axis=-1).astype(np.float32)
        return np.stack([np.cos(ang), np.sin(ang)], axis=1).astype(np.float32)

    return colvalid, swamask, rope_tab(32), rope_tab(64)


def _rpb_expand(rpb):
    cp = np.arange(64)[:, None]
    c = np.arange(64)[None, :]
    dc = np.clip(cp - c + 15, 0, 30)
    drp = 14 - np.arange(15)
    out = rpb[:, drp][:, :, dc]
    return np.ascontiguousarray(out.transpose(2, 0, 1, 3)).reshape(64, 8, 960)


def kernel(x_prompt, x_sample, cache_mla_ckv, cache_mla_krope, cache_na_k, cache_na_v, cache_gqa_k, cache_gqa_v,
           cache_swa_k, cache_swa_v, c, c_ctx, norm_g, w_mod, b_mod, w_in_even, mla_qa_g, w_q_up, mla_kva_g,
           w_kv_up, mla_q_g, mla_k_g, na_q_g, na_k_g, na_rpb, w_out_even, w_in_odd, gqa_q_g, gqa_k_g, swa_q_g,
           swa_k_g, swa_sink, w_out_odd):
    f = lambda a: np.ascontiguousarray(np.asarray(a), dtype=np.float32)
    if "nc" not in _NC_CACHE:
        _NC_CACHE["nc"] = build_program()
    nc = _NC_CACHE["nc"]
    colvalid, swamask, rope_e, rope_o = _consts()
    gains = np.concatenate([f(mla_qa_g)[0], f(mla_kva_g)[0], f(mla_q_g)[0], f(mla_k_g)[0], f(na_q_g)[0], f(na_k_g)[0],
                            f(gqa_q_g)[0], f(gqa_k_g)[0], f(swa_q_g)[0], f(swa_k_g)[0], f(swa_sink)[0]])[None, :]
    shared = {
        "c_ctx": f(c_ctx), "norm_g": f(norm_g), "w_mod": f(w_mod), "b_mod": f(b_mod),
        "w_in_e": f(w_in_even)[0], "w_q_up": f(w_q_up)[0], "w_kv_up": f(w_kv_up)[0], "w_out_e": f(w_out_even)[0],
        "w_in_o": f(w_in_odd)[0], "w_out_o": f(w_out_odd)[0], "gains": f(gains), "rpbT": f(_rpb_expand(f(na_rpb)[0])),
        "colvalid": colvalid, "rope_e": rope_e, "rope_o": rope_o,
    }
    xpf, xsf = f(x_prompt), f(x_sample)
    in_maps = []
    for core in range(NCORES):
        b = core // 4
        m = dict(shared)
        m["xp"] = xpf[core * NPS:(core + 1) * NPS].reshape(NPS * SEQ, D)
        qi = core % 4
        m["swm_own"] = swamask[qi]
        m["own_idx"] = (qi * 256 + np.arange(256, dtype=np.int32)).reshape(256, 1)
        m["rope_own"] = np.ascontiguousarray(rope_o[qi * 256:(qi + 1) * 256])
        m["xs"] = xsf[b]
        m["c_s"] = f(c)[b]
        m["ca_ckv"] = f(cache_mla_ckv)[b, 0]
        m["ca_krope"] = f(cache_mla_krope)[b, 0]
        m["ca_nak"] = f(cache_na_k)[b, 0]
        m["ca_nav"] = f(cache_na_v)[b, 0]
        m["ca_gk"] = f(cache_gqa_k)[b, 0]
        m["ca_gv"] = f(cache_gqa_v)[b, 0]
        m["ca_sk"] = f(cache_swa_k)[b, 0]
        m["ca_sv"] = f(cache_swa_v)[b, 0]
        in_maps.append(m)
    res = run_bass_kernel_spmd(nc, in_maps, core_ids=list(range(NCORES)))
    r = res.results
    cat = lambda name: np.concatenate([r[i][name] for i in range(NCORES)], axis=0)
    y_prompt = cat("yp").reshape(32, SEQ, D)
    y_sample = np.stack([np.concatenate([r[4 * b + qi]["ys"] for qi in range(4)], axis=0) for b in range(2)], axis=0)
    return (y_prompt, y_sample,
            cat("n_ckv")[:, None], cat("n_krope")[:, None], cat("n_nak")[:, None], cat("n_nav")[:, None],
            cat("n_gk")[:, None], cat("n_gv")[:, None], cat("n_sk")[:, None], cat("n_sv")[:, None])
```
